# Optimizing a Trainium2 kernel written in Bass

```python
import jax, jax.numpy as jnp
from jax import lax
import numpy as np

D_MODEL = 1024
BATCH = 8
SEQ = 2048
DEPTH = 1
DEC_BATCH = 128
DEC_SEQ = 1
PAST_LEN = 16384
PAGE_SIZE = 128

RET_DK = 256
RET_HEADS = D_MODEL // 256
RET_DV = 2 * RET_DK
RET_QK = RET_HEADS * RET_DK
RET_V = RET_HEADS * RET_DV
RET_CHUNK = 128
ROPE_BASE = 10000.0
GM_GROUPS = 4
GM_WIDTH = D_MODEL
GM_CG = GM_WIDTH // GM_GROUPS
GM_CHUNK = 128
D_FF = ((8 * D_MODEL // 3 + 255) // 256) * 256
EPS = 1e-6
D_IN = 2 * RET_QK + 2 * RET_V + 2 * GM_WIDTH + 2 * D_MODEL
SPLIT_IDX = (
    RET_QK,
    2 * RET_QK,
    2 * RET_QK + RET_V,
    2 * RET_QK + 2 * RET_V,
    2 * RET_QK + 2 * RET_V + GM_WIDTH,
    2 * RET_QK + 2 * RET_V + 2 * GM_WIDTH,
    2 * RET_QK + 2 * RET_V + 2 * GM_WIDTH + D_MODEL,
)

kernel_name = "retention_gmlp_hybrid_step"


def rms_norm(x, g):
    xf = x.astype(jnp.float32)
    y = xf * lax.rsqrt(jnp.mean(xf * xf, axis=-1, keepdims=True) + EPS)
    return (y * g.astype(jnp.float32)).astype(x.dtype)


def head_rms(x):
    xf = x.astype(jnp.float32)
    return xf * lax.rsqrt(jnp.mean(xf * xf, axis=-1, keepdims=True) + EPS)


def layer_norm(x, g, b):
    xf = x.astype(jnp.float32)
    mu = jnp.mean(xf, axis=-1, keepdims=True)
    xc = xf - mu
    y = xc * lax.rsqrt(jnp.mean(xc * xc, axis=-1, keepdims=True) + EPS)
    return (y * g.astype(jnp.float32) + b.astype(jnp.float32)).astype(x.dtype)


def rotary(x, pos):
    half = x.shape[-1] // 2
    inv = ROPE_BASE ** (-jnp.arange(half, dtype=jnp.float32) / half)
    ang = pos[:, None] * inv[None, :]
    cos = jnp.cos(ang)[None, :, None, :]
    sin = jnp.sin(ang)[None, :, None, :]
    x1 = x[..., :half].astype(jnp.float32)
    x2 = x[..., half:].astype(jnp.float32)
    return jnp.concatenate([x1 * cos - x2 * sin, x1 * sin + x2 * cos], axis=-1)


def retention(q, k, v, state0):
    B, T, H, DK = q.shape
    DV = v.shape[-1]
    L = min(T, RET_CHUNK)
    nc = T // L
    lg = jnp.log1p(-jnp.exp2(-5.0 - jnp.arange(H, dtype=jnp.float32)))
    n = jnp.arange(L, dtype=jnp.float32)
    diff = n[:, None] - n[None, :]
    intra_decay = jnp.where(diff[None] >= 0,
                            jnp.exp(jnp.maximum(diff, 0.0)[None] * lg[:, None, None]), 0.0)
    q_decay = jnp.exp((n + 1.0)[:, None] * lg[None, :])[None, :, :, None]
    k_decay = jnp.exp((L - 1.0 - n)[:, None] * lg[None, :])[None, :, :, None]
    chunk_decay = jnp.exp(L * lg)[None, :, None, None]

    def to_chunks(a):
        return a.reshape(B, nc, L, H, a.shape[-1]).transpose(1, 0, 2, 3, 4)

    def step(state, qkv):
        qc, kc, vc = qkv
        scores = jnp.einsum('bthd,bshd->bhts', qc, kc) * intra_decay[None]
        o = (jnp.einsum('bhts,bshv->bthv', scores, vc)
             + jnp.einsum('bthd,bhdv->bthv', qc, state) * q_decay)
        new_state = chunk_decay * state + jnp.einsum('bthd,bthv->bhdv', kc * k_decay, vc)
        return new_state, o

    state, o = lax.scan(step, state0, (to_chunks(q), to_chunks(k), to_chunks(v)))
    o = o.transpose(1, 0, 2, 3, 4).reshape(B, T, H, DV)
    return o, state


def chunk_gmlp(u, v, gm_ws, gm_bs):
    B, T, _ = v.shape
    L = min(T, GM_CHUNK)
    nc = T // L
    vc = v.reshape(B, nc, L, GM_GROUPS, GM_CG)
    ws = jnp.tril(gm_ws[:, :L, :L])
    bias = gm_bs[:, :L].T[None, None, :, :, None]
    mixed = jnp.einsum('gts,bnsgc->bntgc', ws, vc) + bias
    return u * mixed.reshape(B, T, GM_WIDTH)


def hybrid_layer(x, ret_state0, pos, norm_mix_g, w_in, w_ret_o, gm_ln_g, gm_ln_b,
                 gm_ws, gm_bs, w_gm_o, w_o, norm_ffn_g, w_ffn_in, w_ffn_down):
    B, T, _ = x.shape
    xn = rms_norm(x, norm_mix_g)
    z = xn @ w_in
    q, k, v, g, gu, gv, a_ret, a_gm = jnp.split(z, SPLIT_IDX, axis=-1)

    q = rotary(q.reshape(B, T, RET_HEADS, RET_DK), pos)
    k = rotary(k.reshape(B, T, RET_HEADS, RET_DK), pos) * (RET_DK ** -0.5)
    v = v.reshape(B, T, RET_HEADS, RET_DV).astype(jnp.float32)
    o, new_state = retention(q, k, v, ret_state0.astype(jnp.float32))
    o = head_rms(o).reshape(B, T, RET_V)
    branch_ret = (jax.nn.silu(g.astype(jnp.float32)) * o).astype(x.dtype) @ w_ret_o

    gu = jax.nn.gelu(gu)
    gv = layer_norm(jax.nn.gelu(gv), gm_ln_g, gm_ln_b)
    branch_gm = chunk_gmlp(gu, gv, gm_ws, gm_bs) @ w_gm_o

    m = jax.nn.sigmoid(a_ret) * branch_ret + jax.nn.sigmoid(a_gm) * branch_gm
    h = x + (m @ w_o).astype(x.dtype)

    hn = rms_norm(h, norm_ffn_g)
    f_gate, f_up = jnp.split(hn @ w_ffn_in, 2, axis=-1)
    h = h + ((jax.nn.silu(f_gate) * f_up) @ w_ffn_down).astype(x.dtype)
    return h, new_state, gv


def setup_inputs(seed: int = 0) -> dict:
    key = jax.random.key(seed)
    ks = jax.random.split(key, 20)
    nrm = lambda k, shape, s: jax.random.normal(k, shape, jnp.float32) * s
    return {
        "x_prompt": nrm(ks[0], (BATCH, SEQ, D_MODEL), 1.0),
        "x_sample": nrm(ks[1], (DEC_BATCH, DEC_SEQ, D_MODEL), 1.0),
        "state_ret": nrm(ks[2], (DEPTH, DEC_BATCH, RET_HEADS, RET_DK, RET_DV), 0.5),
        "norm_mix_g": 1.0 + nrm(ks[3], (DEPTH, D_MODEL), 0.05),
        "w_in": nrm(ks[4], (DEPTH, D_MODEL, D_IN), D_MODEL ** -0.5),
        "w_ret_o": nrm(ks[5], (DEPTH, RET_V, D_MODEL), RET_V ** -0.5),
        "gm_ln_g": 1.0 + nrm(ks[6], (DEPTH, GM_WIDTH), 0.05),
        "gm_ln_b": nrm(ks[7], (DEPTH, GM_WIDTH), 0.02),
        "gm_ws": nrm(ks[8], (DEPTH, GM_GROUPS, GM_CHUNK, GM_CHUNK), GM_CHUNK ** -0.5),
        "gm_bs": 1.0 + nrm(ks[9], (DEPTH, GM_GROUPS, GM_CHUNK), 0.1),
        "w_gm_o": nrm(ks[10], (DEPTH, GM_WIDTH, D_MODEL), GM_WIDTH ** -0.5),
        "w_o": nrm(ks[11], (DEPTH, D_MODEL, D_MODEL), D_MODEL ** -0.5),
        "norm_ffn_g": 1.0 + nrm(ks[12], (DEPTH, D_MODEL), 0.05),
        "w_ffn_in": nrm(ks[13], (DEPTH, D_MODEL, 2 * D_FF), D_MODEL ** -0.5),
        "w_ffn_down": nrm(ks[14], (DEPTH, D_FF, D_MODEL), D_FF ** -0.5),
        "norm_final_g": 1.0 + nrm(ks[15], (D_MODEL,), 0.05),
    }


def reference(x_prompt, x_sample, state_ret, norm_mix_g, w_in, w_ret_o, gm_ln_g, gm_ln_b,
              gm_ws, gm_bs, w_gm_o, w_o, norm_ffn_g, w_ffn_in, w_ffn_down, norm_final_g):
    pos_prompt = jnp.arange(SEQ, dtype=jnp.float32)
    pos_sample = PAST_LEN + jnp.arange(DEC_SEQ, dtype=jnp.float32)
    hp, hs = x_prompt, x_sample
    ret_p, ret_s, gmv_s = [], [], []
    for l in range(DEPTH):
        w = (norm_mix_g[l], w_in[l], w_ret_o[l], gm_ln_g[l], gm_ln_b[l], gm_ws[l], gm_bs[l],
             w_gm_o[l], w_o[l], norm_ffn_g[l], w_ffn_in[l], w_ffn_down[l])
        zero_state = jnp.zeros((BATCH, RET_HEADS, RET_DK, RET_DV), jnp.float32)
        hp, sp, _ = hybrid_layer(hp, zero_state, pos_prompt, *w)
        hs, ss, vs = hybrid_layer(hs, state_ret[l], pos_sample, *w)
        ret_p.append(sp)
        ret_s.append(ss)
        gmv_s.append(vs)
    y_prompt = rms_norm(hp, norm_final_g)
    y_sample = rms_norm(hs, norm_final_g)
    ret_state_prompt = jnp.stack(ret_p)
    ret_state_sample = jnp.stack(ret_s)
    gm_v_sample = jnp.stack(gmv_s)
    return (y_prompt, y_sample, ret_state_prompt, ret_state_sample, gm_v_sample)
```

```python
import numpy as np
import ml_dtypes
import concourse.bass as bass
import concourse.mybir as mybir
from concourse.bass_utils import run_bass_kernel_spmd

F32 = mybir.dt.float32
BF16 = mybir.dt.bfloat16
AF = mybir.ActivationFunctionType
ALU = mybir.AluOpType
AX = mybir.AxisListType


class _Op:
    __slots__ = ("eng", "fn", "deps", "idx", "needs_inc", "is_dma", "dsem", "dval", "seq", "rk", "wk")

    def __init__(self, eng, fn):
        self.eng = eng
        self.fn = fn
        self.deps = []
        self.idx = None
        self.needs_inc = False
        self.is_dma = False
        self.dsem = None
        self.dval = 0
        self.seq = 0


class _DSem:
    def __init__(self, name):
        self.name = name
        self.count = 0
        self.handle = None


class FW:
    ENGS = ("pe", "act", "dve", "pool", "sp")

    def __init__(self, nc):
        self.nc = nc
        self.ops = {e: [] for e in self.ENGS}
        self.last_w = {}
        self.readers = {}
        self.dsems = []
        self.ctx = []
        self.nseq = 0
        self.children = {}
        self.marks = []

    def mark(self, name):
        self.marks.append((name, len(self.ops["pe"])))

    def _expand(self, keys):
        out = []
        for k in keys:
            ch = self.children.get(k)
            if ch:
                out.extend(ch)
            else:
                out.append(k)
        return out

    def sbuf(self, name, shape, dt):
        g = self.nc.sbuf_tensor(name, list(shape), dt)
        t = g.__enter__()
        self.ctx.append(g)
        return t

    def psum(self, name, shape, dt):
        g = self.nc.psum_tensor(name, list(shape), dt)
        t = g.__enter__()
        self.ctx.append(g)
        return t

    def dsem(self, name):
        d = _DSem(name)
        self.dsems.append(d)
        return d

    def _track(self, op, reads, writes):
        reads = self._expand(reads)
        writes = self._expand(writes)
        op.rk = list(reads); op.wk = list(writes)
        psr = [k for k in reads if isinstance(k, tuple) and k[0] == "PS"]
        if psr:
            reads = [k for k in reads if not (isinstance(k, tuple) and k[0] == "PS")]
            writes = list(writes) + [k for k in psr if k not in writes]
        deps = []
        for k in reads:
            w = self.last_w.get(k)
            if w is not None:
                deps.append((w, "raw"))
        for k in writes:
            w = self.last_w.get(k)
            if w is not None:
                deps.append((w, "waw"))
            for r in self.readers.get(k, ()):
                deps.append((r, "war"))
        for k in reads:
            self.readers.setdefault(k, []).append(op)
        for k in writes:
            self.last_w[k] = op
            self.readers[k] = []
        seen = set()
        for d, kind in deps:
            if d is op or id(d) in seen:
                continue
            if d.eng == op.eng and not d.is_dma and not op.is_dma:
                if op.eng == "pe":
                    continue
            seen.add(id(d))
            op.deps.append(d)
            if not d.is_dma:
                d.needs_inc = True

    def op(self, eng, fn, reads=(), writes=()):
        o = _Op(eng, fn)
        o.seq = self.nseq
        self.nseq += 1
        self._track(o, reads, writes)
        self.ops[eng].append(o)
        return o

    def dma(self, eng, out, in_, dsem, reads=(), writes=(), **kw):
        o = _Op(eng, None)
        o.is_dma = True
        o.seq = self.nseq
        self.nseq += 1
        o.dsem = dsem
        dsem.count += 16
        o.dval = dsem.count
        o.fn = (out, in_, kw)
        self._track(o, reads, writes)
        self.ops[eng].append(o)
        return o

    def finish(self, final_waits=()):
        nc = self.nc
        sem_g = {}
        sems = {}
        for e in self.ENGS:
            g = nc.semaphore("s_" + e)
            sems[e] = g.__enter__()
            self.ctx.append(g)
        for d in self.dsems:
            g = nc.semaphore("d_" + d.name)
            d.handle = g.__enter__()
            self.ctx.append(g)
        for e in self.ENGS:
            n = 0
            for o in self.ops[e]:
                if o.is_dma:
                    continue
                if o.needs_inc:
                    n += 1
                    o.idx = n
        engobj = {"pe": "tensor", "act": "scalar", "dve": "vector", "pool": "gpsimd", "sp": "sync"}
        self.stats = {}

        def emit_engine(ename, eng):
            waited = {}
            nwait = 0
            for o in self.ops[ename]:
                need = {}
                for d in o.deps:
                    if d.is_dma:
                        key = ("d", id(d.dsem))
                        val = d.dval
                        h = d.dsem.handle
                    else:
                        key = ("e", d.eng)
                        val = d.idx
                        h = sems[d.eng]
                    if waited.get(key, 0) >= val:
                        continue
                    if key not in need or need[key][1] < val:
                        need[key] = (h, val)
                for key, (h, val) in need.items():
                    eng.wait_ge(h, val)
                    waited[key] = val
                    nwait += 1
                if o.is_dma:
                    out, in_, kw = o.fn
                    eng.dma_start(out=out, in_=in_, **kw).then_inc(o.dsem.handle, 16)
                else:
                    ins = o.fn(eng)
                    if o.needs_inc:
                        ins.then_inc(sems[ename], 1)
            if ename == "sp":
                for d in self.dsems:
                    if d.count > 0:
                        eng.wait_ge(d.handle, d.count)
            self.stats[ename] = (len(self.ops[ename]), nwait)

        with nc.Block() as block:
            @block.tensor
            def _(eng):
                emit_engine("pe", eng)

            @block.scalar
            def _(eng):
                emit_engine("act", eng)

            @block.vector
            def _(eng):
                emit_engine("dve", eng)

            @block.gpsimd
            def _(eng):
                emit_engine("pool", eng)

            @block.sync
            def _(eng):
                emit_engine("sp", eng)

        for g in reversed(self.ctx):
            g.__exit__(None, None, None)
        self.ctx = []


D = 1024
SEQ = 2048
NS = 16
DK = 256
DV = 512
H = 4
DFF = 2816
D_IN = 10240
EPS = 1e-6
NCHS = 9
NT = NCHS * 128
GAM = [1.0 - 2.0 ** (-5 - h) for h in range(H)]
CD = [g ** 128 for g in GAM]
C_Q, C_K, C_V, C_G, C_GU, C_GV, C_AR, C_AG = 0, 1024, 2048, 4096, 6144, 7168, 8192, 9216


def _host_consts():
    c = {}
    c["ident"] = np.eye(128, dtype=np.float32).astype(ml_dtypes.bfloat16)
    s = np.arange(128)
    c["maskT"] = (s[None, :] >= s[:, None]).astype(np.float32)
    c["tril"] = (s[None, :] <= s[:, None]).astype(np.float32)
    dec = np.zeros((128, 8), np.float64)
    for h in range(H):
        dec[:, h] = GAM[h] ** (s + 1.0)
        dec[:, 4 + h] = GAM[h] ** (-(s + 1.0)) / 16.0
    c["dec"] = dec.astype(np.float32)
    inv = (np.float32(10000.0) ** (-np.arange(128, dtype=np.float32) / np.float32(128))).astype(np.float32)
    pos = np.concatenate([np.arange(SEQ, dtype=np.float32), np.full((128,), 16384.0, np.float32)])
    ang = (pos[:, None] * inv[None, :]).astype(np.float32)
    c["cs"] = np.concatenate([np.cos(ang), np.sin(ang)], axis=1).astype(np.float32)
    d16 = np.zeros((128, 16, 16), np.float32)
    d16[:, np.arange(16), np.arange(16)] = 1.0
    c["d16"] = d16.reshape(128, 256).astype(ml_dtypes.bfloat16)
    return c


def build_program():
    nc = bass.Bass("TRN2", target_bir_lowering=False)
    fw = FW(nc)

    def din(name, shape, dt=F32):
        return nc.dram_tensor(name, list(shape), dt, kind="ExternalInput").ap()

    def dout(name, shape, dt=F32):
        return nc.dram_tensor(name, list(shape), dt, kind="ExternalOutput").ap()

    xp = din("xp", [SEQ, D]); xs = din("xs", [NS, D]); st_in = din("st_in", [NS, H, DK, DV])
    w_in = din("w_in", [D, D_IN]); w_ret_o = din("w_ret_o", [2048, D]); w_gm_o = din("w_gm_o", [D, D])
    w_o = din("w_o", [D, D]); w_ffn_in = din("w_ffn_in", [D, 2 * DFF]); w_down = din("w_down", [DFF, D])
    g_mix = din("g_mix", [1, D]); g_ffn = din("g_ffn", [1, D]); g_fin = din("g_fin", [1, D])
    ln_g = din("ln_g", [1, D]); ln_b = din("ln_b", [1, D])
    gm_ws = din("gm_ws", [H, 128, 128]); gm_bs = din("gm_bs", [H, 128])
    c_ident = din("ident", [128, 128], BF16); c_maskT = din("maskT", [128, 128]); c_tril = din("tril", [128, 128])
    c_dec = din("dec", [128, 8]); c_cs = din("cs", [SEQ + 128, 256]); c_d16 = din("d16", [128, 256], BF16)
    yp = dout("yp", [SEQ, D]); ys = dout("ys", [NS, D]); rsp = dout("rsp", [H, DK, DV])
    rss = dout("rss", [NS, H, DK, DV]); gmv = dout("gmv", [NS, D])

    sb = fw.sbuf
    RA = sb("RA", [128, 8, NT], BF16)
    RB = sb("RB", [128, 16, NT], BF16)
    RC = sb("RC", [128, 8, NT], BF16)
    RD = sb("RD", [128, NCHS, D], BF16)
    Hf = RB[:].rearrange("p a b -> p (a b)").bitcast(F32).rearrange("p (c n) -> p c n", n=D)
    W = [sb("W%d" % i, [128, 8, 512], BF16) for i in range(6)]
    S2 = [sb("S%d" % h, [128, 2, DV], F32) for h in range(2)]
    Sb = sb("Sb", [128, 2, DV], BF16)
    GA = sb("GA", [128, D], F32); GB = sb("GB", [128, D], F32)
    T4 = [sb("T4%d" % i, [128, D], F32) for i in range(2)]
    T2 = [sb("T2%d" % i, [128, D], BF16) for i in range(3)]
    cst = [sb("cst%d" % i, [128, 256], F32) for i in range(2)]
    ident = sb("identt", [128, 128], BF16); maskT = sb("maskTt", [128, 128], F32)
    wsT = sb("wsT", [128, 4, 128], BF16); bsT = sb("bsT", [128, 4], F32)
    wsb = sb("wsb", [128, 4], F32); bsb = sb("bsb", [128, 4], F32)
    dec = sb("dect", [128, 8], F32); d16 = sb("d16t", [128, 256], BF16)
    mhalf = sb("mhalf", [128, 1], F32)
    stat = sb("stat", [128, 16], F32)
    qks = [sb("qks%d" % i, [128, 4, 128], F32) for i in range(2)]
    rtc = sb("rtc", [128, 4, 128], F32)
    qkr = [sb("qkr%d" % i, [128, 4, 128], BF16) for i in range(2)]
    kdt = [sb("kdt%d" % i, [128, 256], BF16) for i in range(2)]
    qkT = [sb("qkT%d" % i, [128, 4, 128], BF16) for i in range(2)]
    scm = sb("scm", [128, 128], BF16)
    xn0 = sb("xn0", [1, D], F32)
    xn0T = sb("xn0T", [128, 8], F32)
    one11 = sb("one11", [1, 1], F32)
    s00t = sb("s00t", [1, 8], F32)
    vb = [sb("vb%d" % i, [128, DV], BF16) for i in range(2)]
    sg = [sb("sg%d" % i, [128, DV], BF16) for i in range(2)]
    go = [sb("go%d" % i, [128, DV], BF16) for i in range(2)]
    vbs = sb("vbs", [128, DV], BF16); sgs = sb("sgs", [128, DV], BF16)
    ksel = sb("ksel", [16, 16, 128], BF16); qsel = sb("qsel", [128, 2, 16, 16], BF16)
    ksv = sb("ksv", [16, 2, 128], BF16)

    PS = [fw.psum("PS%d" % i, [128, 512], F32) for i in range(8)]
    TR = PS[6][:].bitcast(BF16).rearrange("p (a b) -> p a b", b=128)
    SC = PS[0][:, 0:128]

    class _DC:
        n = 0
    def new_dc():
        _DC.n += 1
        return fw.dsem("c%d" % _DC.n)
    ds_w = [fw.dsem("w%d" % i) for i in range(6)]
    ds_x = [fw.dsem("x%d" % i) for i in range(2)]; ds_cs = [fw.dsem("cs%d" % i) for i in range(2)]
    ds_g = [fw.dsem("ga"), fw.dsem("gb")]
    ds_y = [fw.dsem("y%d" % i) for i in range(2)]
    ds_so = [fw.dsem("so%d" % i) for i in range(H)]; ds_si = [fw.dsem("si%d" % i) for i in range(4)]; ds_sso = [fw.dsem("sso%d" % i) for i in range(4)]
    ds_sl = [fw.dsem("sl%d" % i) for i in range(2)]
    ds_gmv = fw.dsem("gmv")

    fw.children[("T4", 0)] = [("SD", 0), ("SD", 1)]
    fw.children[("T4", 1)] = [("SD", 2), ("SD", 3)]
    fw.children[("T2", 0)] = [("SBF", 0), ("SBF", 1)]
    fw.children[("T2", 1)] = [("SBF", 2), ("SBF", 3)]
    op = fw.op
    cnt = {"w": 0, "x": 0, "cs": 0, "y": 0, "si": 0}
    import os
    DEBUG = bool(os.environ.get("MK_DEBUG"))
    ds_dbg = fw.dsem("dbg")

    def dump(name, ap, shape, dt, keys):
        if not DEBUG:
            return
        t = dout(name, shape, dt)
        for a in range(shape[1]):
            fw.dma("sp", t[:, a, :], ap[:, a, :], ds_dbg, reads=keys)

    fw.dma("sp", ident[:], c_ident[:, :], new_dc(), writes=["ident"])
    fw.dma("sp", maskT[:], c_maskT[:, :], new_dc(), writes=["maskT"])
    fw.dma("sp", dec[:], c_dec[:, :], new_dc(), writes=["dec"])
    fw.dma("sp", d16[:], c_d16[:, :], new_dc(), writes=["d16"])
    fw.dma("sp", T4[0][:, 0:128], c_tril[:, :], new_dc(), writes=[("T4", 0)])
    fw.dma("sp", bsT[:], gm_bs.rearrange("g t -> t g"), new_dc(), writes=["bsT"], allow_slow_non_contiguous=True)
    fw.dma("sp", wsb[:], gm_ws[:, 0:1, 0:1].rearrange("g a b -> (a b) g").partition_broadcast(128), new_dc(), writes=["wsb"], allow_slow_non_contiguous=True)
    fw.dma("sp", bsb[:], gm_bs[:, 0:1].rearrange("g a -> a g").partition_broadcast(128), new_dc(), writes=["bsb"], allow_slow_non_contiguous=True)
    op("dve", lambda e: e.memset(mhalf[:], -0.5), writes=["mhalf"])
    for g in range(H):
        fw.dma("sp", T4[1][:, g * 128:(g + 1) * 128], gm_ws[g, :, :], new_dc(), writes=[("T4", 1)])
    for g in range(H):
        op("dve", lambda e, g=g: e.tensor_tensor(out=T2[0][:, g * 128:(g + 1) * 128], in0=T4[1][:, g * 128:(g + 1) * 128],
                                                   in1=T4[0][:, 0:128], op=ALU.mult), reads=[("T4", 1), ("T4", 0)], writes=[("T2", 0)])
    for g in range(H):
        op("pe", lambda e, g=g: e.transpose(TR[:, g, :], T2[0][:, g * 128:(g + 1) * 128], ident[:]),
           reads=[("T2", 0), "ident"], writes=[("PS", 6)])
    op("act", lambda e: e.copy(out=wsT[:], in_=TR[:, 0:4, :]), reads=[("PS", 6)], writes=["wsT"])

    def load_w(src_list):
        i = cnt["w"] % 6
        cnt["w"] += 1
        for (src, coff, nkc) in src_list:
            ncol = src.shape[1]
            fw.dma("pool", W[i][:, 0:nkc, coff:coff + ncol], src.rearrange("(kc p) n -> p kc n", p=128),
                   ds_w[i], writes=[("W", i)])
        return i

    def load_g(which, src):
        t = GA if which == 0 else GB
        fw.dma("sp", t[:], src[0:1, :].partition_broadcast(128), ds_g[which], writes=[("G", which)])

    def rstd_from_ss(ss_col, out_col, n):
        op("dve", lambda e: e.tensor_scalar(out=stat[:, out_col:out_col + 1], in0=stat[:, ss_col:ss_col + 1],
                                             scalar1=1.0 / n, scalar2=EPS, op0=ALU.mult, op1=ALU.add),
           reads=[("stat", ss_col)], writes=[("stat", out_col)])
        op("pool", lambda e: e.tensor_tensor(out=stat[:, out_col:out_col + 1], in0=stat[:, out_col:out_col + 1],
                                              in1=mhalf[:], op=ALU.pow),
           reads=[("stat", out_col), "mhalf"], writes=[("stat", out_col)])

    def sumsq(in_ap, in_keys, col):
        op("dve", lambda e: e.memset(stat[:, col:col + 1], 0.0), writes=[("stat", col)])
        op("act", lambda e: e.activation(out=T2[2][:, 0:in_ap.shape[1]], in_=in_ap, func=AF.Square,
                                          accum_out=stat[:, col:col + 1]),
           reads=list(in_keys) + [("stat", col)], writes=[("T2", 2), ("stat", col)])

    def norm_chain(src_ap, src_keys, gt_which, xb):
        gt = GA if gt_which == 0 else GB
        sumsq(src_ap, src_keys, 0)
        rstd_from_ss(0, 1, D)
        op("dve", lambda e: e.scalar_tensor_tensor(out=T2[xb][:], in0=src_ap, scalar=stat[:, 1:2], in1=gt[:],
                                                    op0=ALU.mult, op1=ALU.mult),
           reads=list(src_keys) + [("stat", 1), ("G", gt_which)], writes=[("T2", xb)])

    def norm_tr(xb, dstR, dst_key, li):
        for kc in range(8):
            op("pe", lambda e, kc=kc: e.transpose(TR[:, kc, :], T2[xb][:, kc * 128:(kc + 1) * 128], ident[:]),
               reads=[("T2", xb), "ident"], writes=[("PS", 6)])
        op("act", lambda e: e.copy(out=dstR[:, :, li * 128:(li + 1) * 128], in_=TR[:, :, :]),
           reads=[("PS", 6)], writes=[(dst_key, li)])

    def load_x(gi, li):
        i = cnt["x"] % 2
        cnt["x"] += 1
        if li == 8:
            op("dve", lambda e: e.memset(T4[i][:], 0.0), writes=[("T4", i)])
            fw.dma("sp", T4[i][0:NS, :], xs[:, :], ds_x[i], writes=[("T4", i)])
        else:
            c = gi * 8 + li
            fw.dma("sp", T4[i][:], xp[c * 128:(c + 1) * 128, :], ds_x[i], writes=[("T4", i)])
        return i

    def proj(bank, lhs_R, lhs_key, li, slot, nkc=8, kc0=0, start=True, stop=True, ncol=512, coff=0):
        for kc in range(nkc):
            op("pe", lambda e, kc=kc: e.matmul(PS[bank][:, 0:ncol], lhs_R[:, kc0 + kc, li * 128:(li + 1) * 128],
                                                W[slot][:, kc, coff:coff + ncol],
                                                start=(start and kc == 0), stop=(stop and kc == nkc - 1)),
               reads=[(lhs_key, li), ("W", slot)], writes=[("PS", bank)])

    for gi in range(2):
        lis = list(range(9)) if gi == 0 else list(range(8))

        fw.mark('g%d P0' % gi)
        def head_blocks(h):
            a = load_w([(w_in[:, C_Q + h * DK:C_Q + (h + 1) * DK], 0, 8), (w_in[:, C_K + h * DK:C_K + (h + 1) * DK], 256, 8)])
            b = load_w([(w_in[:, C_V + h * DV:C_V + (h + 1) * DV], 0, 8)])
            c = load_w([(w_in[:, C_G + h * DV:C_G + (h + 1) * DV], 0, 8)])
            return a, b, c
        if gi == 0:
            load_g(0, g_mix)
            nxt = head_blocks(0)
            pend = None
            Wf = [W[i][:].rearrange("p a b -> p (a b)").bitcast(F32) for i in range(6)]
            pieces = [(kc, cb) for kc in range(8) for cb in range(4)]
            ds_t0 = [fw.dsem("t0%d" % i) for i in range(3)]

            def t0_load(g8):
                sl = 3 + g8 % 3
                for j in range(4):
                    kc, cb = pieces[g8 * 4 + j]
                    fw.dma("sp", Wf[sl][:, j * 512:(j + 1) * 512], w_in[kc * 128:(kc + 1) * 128, cb * 512:(cb + 1) * 512],
                           ds_t0[sl - 3], writes=[("W", sl)])

            def t0_mm(g8):
                sl = 3 + g8 % 3
                for j in range(4):
                    kc, cb = pieces[g8 * 4 + j]
                    op("pe", lambda e, j=j, kc=kc, cb=cb, sl=sl: e.matmul(PS[cb][0:1, :], xn0T[:, kc:kc + 1], Wf[sl][:, j * 512:(j + 1) * 512],
                                                                       start=(kc == 0), stop=(kc == 7)),
                       reads=["xn0T", ("W", sl)], writes=[("PS", cb)])

            for n_, li in enumerate(lis):
                xi = load_x(gi, li)
                norm_chain(T4[xi][:], [("T4", xi)], 0, n_ % 2)
                if n_ == 0:
                    op("dve", lambda e, xi=xi: e.scalar_tensor_tensor(out=xn0[:], in0=T4[xi][0:1, :], scalar=stat[0:1, 1:2], in1=GA[0:1, :],
                                                                      op0=ALU.mult, op1=ALU.mult),
                       reads=[("T4", xi), ("stat", 1), ("G", 0)], writes=["xn0"])
                    op("dve", lambda e: e.memset(one11[:], 1.0), writes=["one11"])
                    for kc in range(8):
                        op("pe", lambda e, kc=kc: e.matmul(PS[7][:, kc:kc + 1], xn0[0:1, kc * 128:(kc + 1) * 128], one11[:], start=True, stop=True),
                           reads=["xn0", "one11"], writes=[("PS", 7)])
                    op("act", lambda e: e.copy(out=xn0T[:], in_=PS[7][:, 0:8]), reads=[("PS", 7)], writes=["xn0T"])
                    for g8 in range(3):
                        t0_load(g8)
                else:
                    g8 = n_ - 1
                    if g8 < 8:
                        t0_mm(g8)
                        if g8 + 3 < 8:
                            t0_load(g8 + 3)
                if pend is not None:
                    norm_tr(pend[0], RA, "RA", pend[1])
                pend = (n_ % 2, li)
            norm_tr(pend[0], RA, "RA", pend[1])
            for j in range(2):
                op("act", lambda e, j=j: e.copy(out=xn0[:, j * 512:(j + 1) * 512], in_=PS[j][0:1, :]), reads=[("PS", j)], writes=["xn0"])
            for j in range(2):
                op("dve", lambda e, j=j: e.tensor_tensor(out=xn0[:, j * 512:(j + 1) * 512], in0=xn0[:, j * 512:(j + 1) * 512], in1=PS[2 + j][0:1, :], op=ALU.mult),
                   reads=["xn0", ("PS", 2 + j)], writes=["xn0"])
            op("dve", lambda e: e.reduce_sum(out=s00t[:, 0:4], in_=xn0[:].rearrange("p (h d) -> p h d", h=4), axis=AX.X),
               reads=["xn0"], writes=["s00"])
            op("dve", lambda e: e.tensor_scalar(out=s00t[:, 4:8], in0=s00t[:, 0:4], scalar1=1.0 / 16, scalar2=None, op0=ALU.mult),
               reads=["s00"], writes=["s00"])
        else:
            nxt = nxt_g1

        prompt = [li for li in lis if li != 8]
        has_sample = (8 in lis)
        for h in range(H):
            fw.mark('g%d P1 h%d' % (gi, h))
            sQK, sV, sG = nxt
            if h + 1 < H:
                nxt = head_blocks(h + 1)
            Sc = S2[h % 2]
            skey = ("S", h % 2)
            if gi == 0:
                op("dve", lambda e, Sc=Sc: e.memset(Sc[:], 0.0), writes=[skey])
            else:
                fw.dma("sp", Sc[:], rsp[h].rearrange("(hf p) v -> p hf v", p=128), ds_sl[h % 2], reads=[("rsp", h)], writes=[skey])
            op("pool", lambda e, Sc=Sc: e.tensor_copy(out=Sb[:], in_=Sc[:]), reads=[skey], writes=["Sb"])

            def stageA(li, p, h=h, sQK=sQK, sV=sV, sG=sG):
                sample = (li == 8)
                c = 16 if sample else gi * 8 + li
                ci = cnt["cs"] % 2
                cnt["cs"] += 1
                fw.dma("sp", cst[ci][:], c_cs[c * 128:(c + 1) * 128, :], ds_cs[ci], writes=[("cst", ci)])
                proj(0, RA, "RA", li, sQK)
                proj(2, RA, "RA", li, sV)
                proj(4, RA, "RA", li, sG)
                q_in = PS[0][:, 0:256].rearrange("p (a b) -> p a b", b=128)
                k_in = PS[0][:, 256:512].rearrange("p (a b) -> p a b", b=128)
                if sample:
                    op("act", lambda e: e.copy(out=qks[p][:, 0:2, :], in_=q_in), reads=[("PS", 0)], writes=[("qks", p)])
                    op("act", lambda e: e.mul(out=qks[p][:, 2:4, :], in_=k_in, mul=1.0 / 16), reads=[("PS", 0)], writes=[("qks", p)])
                    vt, vk, st_, sk = vbs, "vbs", sgs, "sgs"
                else:
                    op("act", lambda e: e.activation(out=qks[p][:, 0:2, :], in_=q_in, func=AF.Copy, scale=dec[:, h:h + 1]),
                       reads=[("PS", 0), "dec"], writes=[("qks", p)])
                    op("act", lambda e: e.activation(out=qks[p][:, 2:4, :], in_=k_in, func=AF.Copy, scale=dec[:, 4 + h:5 + h]),
                       reads=[("PS", 0), "dec"], writes=[("qks", p)])
                    vt, vk, st_, sk = vb[p], ("vb", p), sg[p], ("sg", p)
                op("act", lambda e: e.copy(out=vt[:], in_=PS[2][:]), reads=[("PS", 2)], writes=[vk])
                op("act", lambda e: e.activation(out=st_[:], in_=PS[4][:], func=AF.Silu), reads=[("PS", 4)], writes=[sk])
                cosb = cst[ci][:, 0:128].unsqueeze(1).to_broadcast([128, 4, 128])
                sinb = cst[ci][:, 128:256].unsqueeze(1).to_broadcast([128, 4, 128])
                op("dve", lambda e: e.tensor_tensor(out=rtc[:], in0=qks[p][:], in1=cosb, op=ALU.mult),
                   reads=[("qks", p), ("cst", ci)], writes=["rtc"])
                op("dve", lambda e: e.tensor_tensor(out=qks[p][:], in0=qks[p][:], in1=sinb, op=ALU.mult),
                   reads=[("qks", p), ("cst", ci)], writes=[("qks", p)])
                c4 = rtc[:].rearrange("p (a b) d -> p a b d", a=2)
                s4 = qks[p][:].rearrange("p (a b) d -> p a b d", a=2)
                o4 = qkr[p][:].rearrange("p (a b) d -> p a b d", a=2)
                op("dve", lambda e: e.tensor_tensor(out=o4[:, :, 0, :], in0=c4[:, :, 0, :], in1=s4[:, :, 1, :], op=ALU.subtract),
                   reads=["rtc", ("qks", p)], writes=[("qkr", p)])
                op("dve", lambda e: e.tensor_tensor(out=o4[:, :, 1, :], in0=s4[:, :, 0, :], in1=c4[:, :, 1, :], op=ALU.add),
                   reads=["rtc", ("qks", p)], writes=[("qkr", p)])
                if not sample:
                    op("act", lambda e: e.mul(out=kdt[p][:], in_=qkr[p][:, 2:4, :].rearrange("p a b -> p (a b)"), mul=CD[h]),
                       reads=[("qkr", p)], writes=[("kdt", p)])

            def stageD(li, p, h=h, np_=128):
                for j in range(4):
                    op("pe", lambda e, j=j: e.transpose(TR[:, j, :], go[p][:, j * 128:(j + 1) * 128], ident[:]), reads=[("go", p), "ident"], writes=[("PS", 6)])
                fence = [("H", q) for q in range(NCHS)] if (h == 0 and li == 0) else []
                op("act", lambda e: e.copy(out=RB[:, h * 4:(h + 1) * 4, li * 128:(li + 1) * 128], in_=TR[:, 0:4, :]),
                   reads=[("PS", 6)], writes=[("RB", li)] + fence)

            def o_evac(p, sgt, sgk, npo, bank):
                op("dve", lambda e: e.memset(stat[:, 2:3], 0.0), writes=[("stat", 2)])
                op("act", lambda e: e.activation(out=T2[2][0:npo, 0:512], in_=PS[bank][0:npo, :], func=AF.Square, accum_out=stat[0:npo, 2:3]),
                   reads=[("PS", bank), ("stat", 2)], writes=[("T2", 2), ("stat", 2)])
                rstd_from_ss(2, 3, DV)
                op("dve", lambda e: e.scalar_tensor_tensor(out=go[p][0:npo, :], in0=PS[bank][0:npo, :], scalar=stat[0:npo, 3:4], in1=sgt[0:npo, :],
                                                            op0=ALU.mult, op1=ALU.mult),
                   reads=[("PS", bank), ("stat", 3), sgk], writes=[("go", p)])

            def stageB(li, p):
                for j in range(4):
                    op("pe", lambda e, j=j: e.transpose(TR[:, j, :], qkr[p][:, j, :], ident[:]), reads=[("qkr", p), "ident"], writes=[("PS", 6)])
                op("act", lambda e: e.copy(out=qkT[p][:], in_=TR[:, 0:4, :]), reads=[("PS", 6)], writes=[("qkT", p)])

            def stageC1(li, p):
                for hf in range(2):
                    op("pe", lambda e, hf=hf: e.matmul(SC, qkT[p][:, 2 + hf, :], qkT[p][:, hf, :], start=(hf == 0), stop=(hf == 1)),
                       reads=[("qkT", p)], writes=[("PS", 0)])
                op("dve", lambda e: e.tensor_tensor(out=scm[:], in0=SC, in1=maskT[:], op=ALU.mult),
                   reads=[("PS", 0), "maskT"], writes=["scm"])
                if gi == 0 and li == 0:
                    op("dve", lambda e, h=h: e.tensor_copy(out=scm[0:1, 0:1], in_=s00t[0:1, 4 + h:5 + h]), reads=["s00", "scm"], writes=["scm"])

            def stageC3(li, p, h=h, Sc=Sc, skey=skey):
                for hf in range(2):
                    bk = 3 if hf == 0 else 7
                    op("pe", lambda e, hf=hf, bk=bk: e.matmul(PS[bk][:], kdt[p][:, hf * 128:(hf + 1) * 128], vb[p][:], start=True, stop=True),
                       reads=[("kdt", p), ("vb", p)], writes=[("PS", bk)])
                for hf in range(2):
                    bk = 3 if hf == 0 else 7
                    op("dve", lambda e, hf=hf, bk=bk: e.scalar_tensor_tensor(out=Sc[:, hf, :], in0=Sc[:, hf, :], scalar=CD[h], in1=PS[bk][:],
                                                                            op0=ALU.mult, op1=ALU.add),
                       reads=[skey, ("PS", bk)], writes=[skey])

            def stageC2(li, p, Sc=Sc, skey=skey):
                op("pe", lambda e: e.matmul(PS[5][:], scm[:], vb[p][:], start=True, stop=False), reads=["scm", ("vb", p)], writes=[("PS", 5)])
                for hf in range(2):
                    op("pe", lambda e, hf=hf: e.matmul(PS[5][:], qkT[p][:, hf, :], Sb[:, hf, :], start=False, stop=(hf == 1)),
                       reads=[("qkT", p), "Sb"], writes=[("PS", 5)])
                op("act", lambda e: e.copy(out=Sb[:], in_=Sc[:]), reads=[skey], writes=["Sb"])
                o_evac(p, sg[p], ("sg", p), 128, 5)

            def tile_bufs(k):
                i4 = k % 4
                sd = T4[i4 // 2][:, (i4 % 2) * 512:(i4 % 2) * 512 + 512]
                sbf = T2[i4 // 2][:, (i4 % 2) * 512:(i4 % 2) * 512 + 512]
                return i4, sd, sbf, ("SD", i4), ("SBF", i4)

            def sample_load(k, h=h):
                hf, b = tiles[k]
                i4, sd, sbf, kd_, kb_ = tile_bufs(k)
                fw.dma("sp", sd, st_in[b, h, hf * 128:(hf + 1) * 128, :], ds_si[i4], writes=[kd_])

            def sample_kv(k, h=h):
                hf, b = tiles[k]
                i4, sd, sbf, kd_, kb_ = tile_bufs(k)
                bk = 3 if k % 2 == 0 else 7
                op("pe", lambda e: e.matmul(PS[bk][:], ksel[:, b, :], vbs[0:16, :], start=True, stop=True),
                   reads=["ksel", "vbs"], writes=[("PS", bk)])
                op("dve", lambda e: e.scalar_tensor_tensor(out=sd, in0=sd, scalar=GAM[h], in1=PS[bk][:], op0=ALU.mult, op1=ALU.add),
                   reads=[kd_, ("PS", bk)], writes=[kd_])
                fw.dma("act", rss[b, h, hf * 128:(hf + 1) * 128, :], sd, ds_sso[i4], reads=[kd_])
                op("act", lambda e: e.copy(out=sbf, in_=sd), reads=[kd_], writes=[kb_])

            def sample_o(k, h=h):
                hf, b = tiles[k]
                i4, sd, sbf, kd_, kb_ = tile_bufs(k)
                first = (k == 0)
                last = (k == len(tiles) - 1)
                op("pe", lambda e: e.matmul(PS[1][0:16, :], qsel[:, hf, b, :], sbf, start=first, stop=last),
                   reads=["qsel", kb_], writes=[("PS", 1)])

            def make_ksel(hf, p):
                op("dve", lambda e: e.tensor_tensor(out=ksel[:, :, :], in0=qkr[p][0:16, 2 + hf, :].unsqueeze(1).to_broadcast([16, 16, 128]),
                                                    in1=ident[0:16, 0:16].unsqueeze(2).to_broadcast([16, 16, 128]), op=ALU.mult),
                   reads=[("qkr", p), "ident"], writes=["ksel"])

            tiles = []
            if has_sample:
                stageA(8, 1)
                for hf in range(2):
                    op("pe", lambda e, hf=hf: e.transpose(TR[:, hf, :], qkr[1][:, hf, :], ident[:]), reads=[("qkr", 1), "ident"], writes=[("PS", 6)])
                op("act", lambda e: e.copy(out=qkT[1][:, 0:2, :], in_=TR[:, 0:2, :]), reads=[("PS", 6)], writes=[("qkT", 1)])
                d16v = d16[:].rearrange("p (a b) -> p a b", a=16)
                for hf in range(2):
                    op("dve", lambda e, hf=hf: e.tensor_tensor(out=qsel[:, hf, :, :], in0=qkT[1][:, hf, 0:16].unsqueeze(1).to_broadcast([128, 16, 16]),
                                                               in1=d16v, op=ALU.mult),
                       reads=[("qkT", 1), "d16"], writes=["qsel"])
                op("dve", lambda e: e.tensor_copy(out=ksv[:], in_=qkr[1][0:16, 2:4, :]), reads=[("qkr", 1)], writes=["ksv"])
                tiles = [(hf, b) for hf in range(2) for b in range(NS)]

            def make_ksel2(hf):
                op("dve", lambda e: e.tensor_tensor(out=ksel[:, :, :], in0=ksv[:, hf, :].unsqueeze(1).to_broadcast([16, 16, 128]),
                                                    in1=ident[0:16, 0:16].unsqueeze(2).to_broadcast([16, 16, 128]), op=ALU.mult),
                   reads=["ksv", "ident"], writes=["ksel"])

            npr = len(prompt)
            per_it = (len(tiles) + npr - 1) // npr if tiles else 0
            for k in range(min(4, len(tiles))):
                sample_load(k)
            stageA(prompt[0], 0)
            prev = None
            tpos = 0
            pending_o = []
            for idx, li in enumerate(prompt):
                p = idx % 2
                stageB(li, p)
                if prev is not None:
                    stageC2(prev[0], prev[1])
                if idx + 1 < npr:
                    stageA(prompt[idx + 1], (idx + 1) % 2)
                stageC1(li, p)
                if prev is not None:
                    stageD(prev[0], prev[1])
                stageC3(li, p)
                prev = (li, p)
                for k in pending_o:
                    sample_o(k)
                pending_o = []
                for _ in range(per_it):
                    if tpos < len(tiles):
                        if tiles[tpos][1] == 0:
                            make_ksel2(tiles[tpos][0])
                        sample_kv(tpos)
                        if tpos + 4 < len(tiles):
                            sample_load(tpos + 4)
                        pending_o.append(tpos)
                        tpos += 1
            for k in pending_o:
                sample_o(k)
            stageC2(prev[0], prev[1])
            stageD(prev[0], prev[1])
            if has_sample:
                pz = npr % 2
                op("dve", lambda e: e.memset(go[pz][:], 0.0), writes=[("go", pz)])
                o_evac(pz, sgs, "sgs", NS, 1)
                stageD(8, pz)
            fw.dma("sp", rsp[h].rearrange("(hf p) v -> p hf v", p=128), Sc[:], ds_so[h], reads=[skey], writes=[("rsp", h)])

        if gi == 0:
            dump("dbg_go", RB[:], [128, 16, NT], BF16, [("RB", q) for q in range(NCHS)])
        fw.mark('g%d P2a' % gi)
        load_g(0, ln_g); load_g(1, ln_b)
        sGU = [load_w([(w_in[:, C_GU + j * 512:C_GU + (j + 1) * 512], 0, 8)]) for j in range(2)]
        sGV = [load_w([(w_in[:, C_GV + j * 512:C_GV + (j + 1) * 512], 0, 8)]) for j in range(2)]
        sWGM = [load_w([(w_gm_o[:, j * 512:(j + 1) * 512], 0, 8)]) for j in range(2)]
        ggu_b = [T2[0][:], qks[0][:].rearrange("p a b -> p (a b)").bitcast(BF16)]
        ggu_k = [("T2", 0), ("qks", 0)]
        vn_b = [T2[1][:], qks[1][:].rearrange("p a b -> p (a b)").bitcast(BF16)]
        vn_k = [("T2", 1), ("qks", 1)]
        junk41 = T4[1][:].bitcast(BF16)[:, 0:D]

        def p2a_Xpe(li):
            for j in range(2):
                proj(j, RA, "RA", li, sGU[j])
            for j in range(2):
                proj(2 + j, RA, "RA", li, sGV[j])

        def p2a_Xrest(li, q):
            sample = (li == 8)
            gu_t, gu_key, vn_t, vn_key = ggu_b[q], ggu_k[q], vn_b[q], vn_k[q]
            for j in range(2):
                op("act", lambda e, j=j: e.activation(out=gu_t[:, j * 512:(j + 1) * 512], in_=PS[j][:], func=AF.Gelu_apprx_tanh),
                   reads=[("PS", j)], writes=[gu_key])
            op("dve", lambda e: e.memset(stat[:, 4:8], 0.0), writes=[("stat", 4), ("stat", 6), ("stat", 7)])
            for j in range(2):
                op("act", lambda e, j=j: e.activation(out=T4[0][:, j * 512:(j + 1) * 512], in_=PS[2 + j][:], func=AF.Gelu_apprx_tanh,
                                                      accum_out=stat[:, 4 + j:5 + j]),
                   reads=[("PS", 2 + j), ("stat", 4)], writes=[("T4", 0), ("stat", 4)])
            op("act", lambda e: e.activation(out=vn_t, in_=T4[0][:], func=AF.Square, accum_out=stat[:, 6:7]),
               reads=[("T4", 0), ("stat", 6)], writes=[vn_key, ("stat", 6)])
            op("dve", lambda e: e.tensor_tensor(out=stat[:, 7:8], in0=stat[:, 4:5], in1=stat[:, 5:6], op=ALU.add), reads=[("stat", 4)], writes=[("stat", 7)])
            op("dve", lambda e: e.tensor_scalar(out=stat[:, 8:9], in0=stat[:, 7:8], scalar1=1.0 / D, scalar2=None, op0=ALU.mult),
               reads=[("stat", 7)], writes=[("stat", 8)])
            op("dve", lambda e: e.scalar_tensor_tensor(out=T4[1][:], in0=T4[0][:], scalar=stat[:, 8:9], in1=GA[:], op0=ALU.subtract, op1=ALU.mult),
               reads=[("T4", 0), ("stat", 8), ("G", 0)], writes=[("T4", 1)])
            op("dve", lambda e: e.tensor_tensor(out=stat[:, 10:11], in0=stat[:, 8:9], in1=stat[:, 8:9], op=ALU.mult), reads=[("stat", 8)], writes=[("stat", 10)])
            op("dve", lambda e: e.scalar_tensor_tensor(out=stat[:, 9:10], in0=stat[:, 6:7], scalar=1.0 / D, in1=stat[:, 10:11],
                                                        op0=ALU.mult, op1=ALU.subtract), reads=[("stat", 6), ("stat", 10)], writes=[("stat", 9)])
            op("dve", lambda e: e.tensor_scalar(out=stat[:, 9:10], in0=stat[:, 9:10], scalar1=EPS, scalar2=None, op0=ALU.add),
               reads=[("stat", 9)], writes=[("stat", 9)])
            op("pool", lambda e: e.tensor_tensor(out=stat[:, 9:10], in0=stat[:, 9:10], in1=mhalf[:], op=ALU.pow),
               reads=[("stat", 9), "mhalf"], writes=[("stat", 9)])
            if not sample:
                op("dve", lambda e: e.scalar_tensor_tensor(out=vn_t, in0=T4[1][:], scalar=stat[:, 9:10], in1=GB[:], op0=ALU.mult, op1=ALU.add),
                   reads=[("T4", 1), ("stat", 9), ("G", 1)], writes=[vn_key])
            else:
                op("dve", lambda e: e.scalar_tensor_tensor(out=T4[0][:], in0=T4[1][:], scalar=stat[:, 9:10], in1=GB[:], op0=ALU.mult, op1=ALU.add),
                   reads=[("T4", 1), ("stat", 9), ("G", 1)], writes=[("T4", 0)])
                fw.dma("sp", gmv[:, :], T4[0][0:NS, :], ds_gmv, reads=[("T4", 0)])

        def p2a_Ymm_pe(li, q):
            vn_t, vn_key = vn_b[q], vn_k[q]
            if li != 8:
                for g in range(4):
                    op("pe", lambda e, g=g: e.matmul(PS[4 + g // 2][:, (g % 2) * 256:(g % 2) * 256 + 256], wsT[:, g, :], vn_t[:, g * 256:(g + 1) * 256],
                                                     start=True, stop=True), reads=["wsT", vn_key], writes=[("PS", 4 + g // 2)])

        def p2a_Ymm_dve(li, q):
            sample = (li == 8)
            gu_t, gu_key, vn_t, vn_key = ggu_b[q], ggu_k[q], vn_b[q], vn_k[q]
            if not sample:
                for g in range(4):
                    op("dve", lambda e, g=g: e.scalar_tensor_tensor(out=RD[:, li, g * 256:(g + 1) * 256], in0=PS[4 + g // 2][:, (g % 2) * 256:(g % 2) * 256 + 256],
                                                                    scalar=bsT[:, g:g + 1], in1=gu_t[:, g * 256:(g + 1) * 256], op0=ALU.add, op1=ALU.mult),
                       reads=[("PS", 4 + g // 2), "bsT", gu_key], writes=[("RD", li)])
            else:
                for g in range(4):
                    op("dve", lambda e, g=g: e.tensor_scalar(out=T4[1][:, g * 256:(g + 1) * 256], in0=T4[0][:, g * 256:(g + 1) * 256],
                                                             scalar1=wsb[:, g:g + 1], scalar2=bsb[:, g:g + 1], op0=ALU.mult, op1=ALU.add),
                       reads=[("T4", 0), "wsb", "bsb"], writes=[("T4", 1)])
                op("dve", lambda e: e.tensor_tensor(out=RD[:, li, :], in0=T4[1][:], in1=gu_t, op=ALU.mult),
                   reads=[("T4", 1), gu_key], writes=[("RD", li)])

        def p2a_Ytr(li):
            for kc in range(8):
                op("pe", lambda e, kc=kc: e.transpose(TR[:, kc, :], RD[:, li, kc * 128:(kc + 1) * 128], ident[:]), reads=[("RD", li), "ident"], writes=[("PS", 6)])
            op("act", lambda e: e.copy(out=RC[:, :, li * 128:(li + 1) * 128], in_=TR[:, :, :]), reads=[("PS", 6)], writes=[("RC", li)])

        p2a_Xpe(lis[0])
        p2a_Xrest(lis[0], 0)
        for n_, li in enumerate(lis):
            nx = n_ + 1 < len(lis)
            if nx:
                p2a_Xpe(lis[n_ + 1])
            p2a_Ymm_pe(li, n_ % 2)
            if n_ > 0:
                p2a_Ytr(lis[n_ - 1])
            p2a_Ymm_dve(li, n_ % 2)
            if nx:
                p2a_Xrest(lis[n_ + 1], (n_ + 1) % 2)
        p2a_Ytr(lis[-1])

        if gi == 0:
            dump("dbg_gm", RC[:], [128, 8, NT], BF16, [("RC", q) for q in range(NCHS)])
        fw.mark('g%d P2b' % gi)
        sAGM = [load_w([(w_in[:, C_AG + j * 512:C_AG + (j + 1) * 512], 0, 8)]) for j in range(2)]
        sWRO0 = [load_w([(w_ret_o[0:1024, j * 512:(j + 1) * 512], 0, 8)]) for j in range(2)]
        for n_, li in enumerate(lis):
            bb = 4 * (n_ % 2)
            tp = n_ % 2
            for j in range(2):
                proj(bb + j, RC, "RC", li, sWGM[j])
            for j in range(2):
                proj(bb + 2 + j, RA, "RA", li, sAGM[j])
            for j in range(2):
                op("act", lambda e, j=j, bb=bb, tp=tp: e.activation(out=T4[tp][:, j * 512:(j + 1) * 512], in_=PS[bb + 2 + j][:], func=AF.Sigmoid),
                   reads=[("PS", bb + 2 + j)], writes=[("T4", tp)])
                op("dve", lambda e, j=j, li=li, bb=bb, tp=tp: e.tensor_tensor(out=RD[:, li, j * 512:(j + 1) * 512], in0=PS[bb + j][:], in1=T4[tp][:, j * 512:(j + 1) * 512], op=ALU.mult),
                   reads=[("PS", bb + j), ("T4", tp)], writes=[("RD", li)])

        if gi == 0:
            dump("dbg_mgm", RD[:], [128, NCHS, D], BF16, [("RD", q) for q in range(NCHS)])
        fw.mark('g%d P3a' % gi)
        sWRO = [sWRO0, [load_w([(w_ret_o[1024:2048, j * 512:(j + 1) * 512], 0, 8)]) for j in range(2)]]
        sAR = [load_w([(w_in[:, C_AR + j * 512:C_AR + (j + 1) * 512], 0, 8)]) for j in range(2)]
        for n_, li in enumerate(lis):
            bb = 4 * (n_ % 2)
            tp = n_ % 2
            for j in range(2):
                proj(bb + j, RB, "RB", li, sWRO[0][j], kc0=0, start=True, stop=False)
                proj(bb + j, RB, "RB", li, sWRO[1][j], kc0=8, start=False, stop=True)
                proj(bb + 2 + j, RA, "RA", li, sAR[j])
            for j in range(2):
                op("act", lambda e, j=j, bb=bb, tp=tp: e.activation(out=T4[tp][:, j * 512:(j + 1) * 512], in_=PS[bb + 2 + j][:], func=AF.Sigmoid),
                   reads=[("PS", bb + 2 + j)], writes=[("T4", tp)])
                op("dve", lambda e, j=j, bb=bb, tp=tp: e.tensor_tensor(out=T4[tp][:, j * 512:(j + 1) * 512], in0=PS[bb + j][:], in1=T4[tp][:, j * 512:(j + 1) * 512], op=ALU.mult),
                   reads=[("PS", bb + j), ("T4", tp)], writes=[("T4", tp)])
            op("dve", lambda e, li=li, tp=tp: e.tensor_tensor(out=RD[:, li, :], in0=T4[tp][:], in1=RD[:, li, :], op=ALU.add),
               reads=[("T4", tp), ("RD", li)], writes=[("RD", li)])

        if gi == 0:
            dump("dbg_m", RD[:], [128, NCHS, D], BF16, [("RD", q) for q in range(NCHS)])
        fw.mark('g%d P3b' % gi)
        sWO = [load_w([(w_o[:, j * 512:(j + 1) * 512], 0, 8)]) for j in range(2)]
        load_g(0, g_ffn)

        def ffn_in_blocks(part):
            f0 = part * 1024
            nf = min(1024, DFF - f0)
            blocks = []
            for j in range((nf + 511) // 512):
                ncol = min(512, nf - j * 512)
                sg_ = load_w([(w_ffn_in[:, f0 + j * 512:f0 + j * 512 + ncol], 0, 8)])
                su_ = load_w([(w_ffn_in[:, DFF + f0 + j * 512:DFF + f0 + j * 512 + ncol], 0, 8)])
                blocks.append((sg_, su_, ncol))
            return blocks
        blocks_next = ffn_in_blocks(0)
        mTb = [T2[0][:].rearrange("p (a b) -> p a b", b=128),
               qks[0][:].rearrange("p a b -> p (a b)").bitcast(BF16).rearrange("p (a b) -> p a b", b=128)]
        mTk = [("T2", 0), ("qks", 0)]
        TR7 = PS[7][:].bitcast(BF16).rearrange("p (a b) -> p a b", b=128)

        def p3b_tm(li, q):
            mt_, mk_ = mTb[q], mTk[q]
            for kc in range(8):
                op("pe", lambda e, kc=kc: e.transpose(TR7[:, kc, :], RD[:, li, kc * 128:(kc + 1) * 128], ident[:]), reads=[("RD", li), "ident"], writes=[("PS", 7)])
            op("act", lambda e: e.copy(out=mt_, in_=TR7[:, :, :]), reads=[("PS", 7)], writes=[mk_])

        def p3b_mm(li, q):
            mt_, mk_ = mTb[q], mTk[q]
            for j in range(2):
                for kc in range(8):
                    op("pe", lambda e, kc=kc, j=j, sl=sWO[j]: e.matmul(PS[4 + j][:], mt_[:, kc, :], W[sl][:, kc, :], start=(kc == 0), stop=(kc == 7)),
                       reads=[mk_, ("W", sWO[j])], writes=[("PS", 4 + j)])

        def p3b_post(li):
            xi = load_x(gi, li)
            for j in range(2):
                rbkeys = [("RB", q) for q in range(NCHS)] if (li == 0 and j == 0) else []
                op("dve", lambda e, j=j, xi=xi: e.tensor_tensor(out=Hf[:, li, j * 512:(j + 1) * 512], in0=PS[4 + j][:], in1=T4[xi][:, j * 512:(j + 1) * 512], op=ALU.add),
                   reads=[("PS", 4 + j), ("T4", xi)], writes=[("H", li)] + rbkeys)
            norm_chain(Hf[:, li, :], [("H", li)], 0, 1)

        p3b_tm(lis[0], 0)
        if len(lis) > 1:
            p3b_tm(lis[1], 1)
        p3b_mm(lis[0], 0)
        p3b_post(lis[0])
        for n_, li in enumerate(lis):
            if n_ + 1 < len(lis):
                p3b_mm(lis[n_ + 1], (n_ + 1) % 2)
            if n_ + 2 < len(lis):
                p3b_tm(lis[n_ + 2], n_ % 2)
            norm_tr(1, RA, "RA", li)
            if n_ + 1 < len(lis):
                p3b_post(lis[n_ + 1])

        if gi == 0:
            dump("dbg_h", Hf, [128, NCHS, D], F32, [("H", q) for q in range(NCHS)])
            dump("dbg_hn", RA[:], [128, 8, NT], BF16, [("RA", q) for q in range(NCHS)])
        for part in range(3):
            fw.mark('g%d FFN%d' % (gi, part))
            f0 = part * 1024
            nf = min(1024, DFF - f0)
            nfg = nf // 128
            blocks = blocks_next
            sWD = [load_w([(w_down[f0:f0 + nf, j * 512:(j + 1) * 512], 0, nfg)]) for j in range(2)]
            ntok = len(lis) * 128
            tts = [(t0, min(512, ntok - t0)) for t0 in range(0, ntok, 512)]
            pair = 0
            for fg in range(nfg):
                sg_, su_, ncol = blocks[fg // 4]
                co = (fg % 4) * 128
                for (t0, tn) in tts:
                    bg, bu = [(0, 1), (2, 3), (4, 5)][pair % 3]
                    pair += 1
                    rkeys = [("RA", q) for q in range(t0 // 128, (t0 + tn) // 128)]
                    for kc in range(8):
                        op("pe", lambda e, kc=kc, bg=bg, sg_=sg_, co=co, t0=t0, tn=tn: e.matmul(PS[bg][:, 0:tn], W[sg_][:, kc, co:co + 128], RA[:, kc, t0:t0 + tn],
                                                                                                start=(kc == 0), stop=(kc == 7)),
                           reads=rkeys + [("W", sg_)], writes=[("PS", bg)])
                    for kc in range(8):
                        op("pe", lambda e, kc=kc, bu=bu, su_=su_, co=co, t0=t0, tn=tn: e.matmul(PS[bu][:, 0:tn], W[su_][:, kc, co:co + 128], RA[:, kc, t0:t0 + tn],
                                                                                                start=(kc == 0), stop=(kc == 7)),
                           reads=rkeys + [("W", su_)], writes=[("PS", bu)])
                    ti = pair % 2
                    op("act", lambda e, bg=bg, tn=tn, ti=ti: e.activation(out=T4[ti][:, 0:tn], in_=PS[bg][:, 0:tn], func=AF.Silu),
                       reads=[("PS", bg)], writes=[("T4", ti)])
                    op("dve", lambda e, bu=bu, tn=tn, t0=t0, fg=fg, ti=ti: e.tensor_tensor(out=RC[:, fg, t0:t0 + tn], in0=PS[bu][:, 0:tn], in1=T4[ti][:, 0:tn], op=ALU.mult),
                       reads=[("PS", bu), ("T4", ti)], writes=[("RC", q) for q in range(t0 // 128, (t0 + tn) // 128)])
            last = (part == 2)
            if not last:
                blocks_next = ffn_in_blocks(part + 1)
            elif gi == 0:
                load_g(0, g_mix)
                nxt_g1 = head_blocks(0)
            if part == 1:
                load_g(1, g_fin)
            st = {'n': 0, 'pend': None}
            fin_prev = [None]

            def fin_p0(li, st=st):
                    sumsq(Hf[:, li, :], [("H", li)], 0)
                    rstd_from_ss(0, 1, D)
                    yi = cnt["y"] % 2
                    cnt["y"] += 1
                    op("dve", lambda e, li=li, yi=yi: e.scalar_tensor_tensor(out=T4[yi][:], in0=Hf[:, li, :], scalar=stat[:, 1:2], in1=GB[:], op0=ALU.mult, op1=ALU.mult),
                       reads=[("H", li), ("stat", 1), ("G", 1)], writes=[("T4", yi)])
                    if li == 8:
                        fw.dma("sp", ys[:, :], T4[yi][0:NS, :], ds_y[yi], reads=[("T4", yi)])
                    else:
                        c = gi * 8 + li
                        fw.dma("sp", yp[c * 128:(c + 1) * 128, :], T4[yi][:], ds_y[yi], reads=[("T4", yi)])
                    if gi == 0 and li < 8:
                        xi = load_x(1, li)
                        norm_chain(T4[xi][:], [("T4", xi)], 0, st['n'] % 2)
                        if st['pend'] is not None:
                            norm_tr(st['pend'][0], RA, "RA", st['pend'][1])
                        st['pend'] = (st['n'] % 2, li)
                        st['n'] += 1

            for li in lis:
                for j in range(2):
                    bk = (0, 1, 2, 3)[(li % 2) * 2 + j]
                    for fg in range(nfg):
                        op("pe", lambda e, fg=fg, j=j, bk=bk, li=li, sl=sWD[j], nfg=nfg: e.matmul(PS[bk][:], RC[:, fg, li * 128:(li + 1) * 128], W[sl][:, fg, :],
                                                                              start=(fg == 0), stop=(fg == nfg - 1)),
                           reads=[("RC", li), ("W", sWD[j])], writes=[("PS", bk)])
                    op("dve", lambda e, j=j, bk=bk, li=li: e.tensor_tensor(out=Hf[:, li, j * 512:(j + 1) * 512], in0=PS[bk][:], in1=Hf[:, li, j * 512:(j + 1) * 512], op=ALU.add),
                       reads=[("PS", bk), ("H", li)], writes=[("H", li)])
                if last:
                    if fin_prev[0] is not None:
                        fin_p0(fin_prev[0])
                    fin_prev[0] = li
            if last:
                fin_p0(fin_prev[0])
            if last and gi == 0:
                norm_tr(st['pend'][0], RA, "RA", st['pend'][1])

        if gi == 0:
            for li in range(NCHS):
                pass
        fw.alias_fence = True

    fw.finish()
    return nc, fw


_CACHE = {}


def kernel(**inputs):
    f32 = np.float32
    consts = _host_consts()
    if "nc" not in _CACHE:
        _CACHE["nc"] = build_program()
    nc, fw = _CACHE["nc"]
    x_prompt = np.asarray(inputs["x_prompt"], f32)
    x_sample = np.asarray(inputs["x_sample"], f32)
    state_ret = np.asarray(inputs["state_ret"], f32)
    shared = {
        "w_in": np.ascontiguousarray(np.asarray(inputs["w_in"], f32)[0]),
        "w_ret_o": np.ascontiguousarray(np.asarray(inputs["w_ret_o"], f32)[0]),
        "w_gm_o": np.ascontiguousarray(np.asarray(inputs["w_gm_o"], f32)[0]),
        "w_o": np.ascontiguousarray(np.asarray(inputs["w_o"], f32)[0]),
        "w_ffn_in": np.ascontiguousarray(np.asarray(inputs["w_ffn_in"], f32)[0]),
        "w_down": np.ascontiguousarray(np.asarray(inputs["w_ffn_down"], f32)[0]),
        "g_mix": np.asarray(inputs["norm_mix_g"], f32).reshape(1, D),
        "g_ffn": np.asarray(inputs["norm_ffn_g"], f32).reshape(1, D),
        "g_fin": np.asarray(inputs["norm_final_g"], f32).reshape(1, D),
        "ln_g": np.asarray(inputs["gm_ln_g"], f32).reshape(1, D),
        "ln_b": np.asarray(inputs["gm_ln_b"], f32).reshape(1, D),
        "gm_ws": np.ascontiguousarray(np.asarray(inputs["gm_ws"], f32)[0]),
        "gm_bs": np.ascontiguousarray(np.asarray(inputs["gm_bs"], f32)[0]),
    }
    shared.update(consts)
    in_maps = []
    for c in range(8):
        m = dict(shared)
        m["xp"] = np.ascontiguousarray(x_prompt[c])
        m["xs"] = np.ascontiguousarray(x_sample[c * NS:(c + 1) * NS, 0, :])
        m["st_in"] = np.ascontiguousarray(state_ret[0, c * NS:(c + 1) * NS])
        in_maps.append(m)
    res = run_bass_kernel_spmd(nc, in_maps, core_ids=list(range(8)))
    r = res.results
    _CACHE["last"] = r
    y_prompt = np.stack([np.asarray(r[c]["yp"], f32) for c in range(8)], axis=0)
    y_sample = np.concatenate([np.asarray(r[c]["ys"], f32) for c in range(8)], axis=0).reshape(128, 1, D)
    ret_p = np.stack([np.asarray(r[c]["rsp"], f32) for c in range(8)], axis=0)[None]
    ret_s = np.concatenate([np.asarray(r[c]["rss"], f32) for c in range(8)], axis=0)[None]
    gm_v = np.concatenate([np.asarray(r[c]["gmv"], f32) for c in range(8)], axis=0).reshape(1, 128, 1, D)
    return (y_prompt, y_sample, ret_p, ret_s, gm_v)
```

```python
import numpy as np
import ml_dtypes
import concourse.bass as bass
import concourse.mybir as mybir
from concourse.bass_utils import run_bass_kernel_spmd

F32 = mybir.dt.float32
BF16 = mybir.dt.bfloat16
AF = mybir.ActivationFunctionType
ALU = mybir.AluOpType
AX = mybir.AxisListType


class _Op:
    __slots__ = ("eng", "fn", "deps", "idx", "needs_inc", "is_dma", "dsem", "dval", "seq", "rk", "wk")

    def __init__(self, eng, fn):
        self.eng = eng
        self.fn = fn
        self.deps = []
        self.idx = None
        self.needs_inc = False
        self.is_dma = False
        self.dsem = None
        self.dval = 0
        self.seq = 0


class _DSem:
    def __init__(self, name):
        self.name = name
        self.count = 0
        self.handle = None


class FW:
    ENGS = ("pe", "act", "dve", "pool", "sp")

    def __init__(self, nc):
        self.nc = nc
        self.ops = {e: [] for e in self.ENGS}
        self.last_w = {}
        self.readers = {}
        self.dsems = []
        self.ctx = []
        self.nseq = 0
        self.children = {}
        self.marks = []

    def mark(self, name):
        self.marks.append((name, len(self.ops["pe"])))

    def _expand(self, keys):
        out = []
        for k in keys:
            ch = self.children.get(k)
            if ch:
                out.extend(ch)
            else:
                out.append(k)
        return out

    def sbuf(self, name, shape, dt):
        g = self.nc.sbuf_tensor(name, list(shape), dt)
        t = g.__enter__()
        self.ctx.append(g)
        return t

    def psum(self, name, shape, dt):
        g = self.nc.psum_tensor(name, list(shape), dt)
        t = g.__enter__()
        self.ctx.append(g)
        return t

    def dsem(self, name):
        d = _DSem(name)
        self.dsems.append(d)
        return d

    def _track(self, op, reads, writes):
        reads = self._expand(reads)
        writes = self._expand(writes)
        op.rk = list(reads); op.wk = list(writes)
        psr = [k for k in reads if isinstance(k, tuple) and k[0] == "PS"]
        if psr:
            reads = [k for k in reads if not (isinstance(k, tuple) and k[0] == "PS")]
            writes = list(writes) + [k for k in psr if k not in writes]
        deps = []
        for k in reads:
            w = self.last_w.get(k)
            if w is not None:
                deps.append((w, "raw"))
        for k in writes:
            w = self.last_w.get(k)
            if w is not None:
                deps.append((w, "waw"))
            for r in self.readers.get(k, ()):
                deps.append((r, "war"))
        for k in reads:
            self.readers.setdefault(k, []).append(op)
        for k in writes:
            self.last_w[k] = op
            self.readers[k] = []
        seen = set()
        for d, kind in deps:
            if d is op or id(d) in seen:
                continue
            if d.eng == op.eng and not d.is_dma and not op.is_dma:
                if op.eng == "pe":
                    continue
            seen.add(id(d))
            op.deps.append(d)
            if not d.is_dma:
                d.needs_inc = True

    def op(self, eng, fn, reads=(), writes=()):
        o = _Op(eng, fn)
        o.seq = self.nseq
        self.nseq += 1
        self._track(o, reads, writes)
        self.ops[eng].append(o)
        return o

    def dma(self, eng, out, in_, dsem, reads=(), writes=(), **kw):
        o = _Op(eng, None)
        o.is_dma = True
        o.seq = self.nseq
        self.nseq += 1
        o.dsem = dsem
        dsem.count += 16
        o.dval = dsem.count
        o.fn = (out, in_, kw)
        self._track(o, reads, writes)
        self.ops[eng].append(o)
        return o

    def finish(self, final_waits=()):
        nc = self.nc
        sem_g = {}
        sems = {}
        for e in self.ENGS:
            g = nc.semaphore("s_" + e)
            sems[e] = g.__enter__()
            self.ctx.append(g)
        for d in self.dsems:
            g = nc.semaphore("d_" + d.name)
            d.handle = g.__enter__()
            self.ctx.append(g)
        for e in self.ENGS:
            n = 0
            for o in self.ops[e]:
                if o.is_dma:
                    continue
                if o.needs_inc:
                    n += 1
                    o.idx = n
        engobj = {"pe": "tensor", "act": "scalar", "dve": "vector", "pool": "gpsimd", "sp": "sync"}
        self.stats = {}

        def emit_engine(ename, eng):
            waited = {}
            nwait = 0
            for o in self.ops[ename]:
                need = {}
                for d in o.deps:
                    if d.is_dma:
                        key = ("d", id(d.dsem))
                        val = d.dval
                        h = d.dsem.handle
                    else:
                        key = ("e", d.eng)
                        val = d.idx
                        h = sems[d.eng]
                    if waited.get(key, 0) >= val:
                        continue
                    if key not in need or need[key][1] < val:
                        need[key] = (h, val)
                for key, (h, val) in need.items():
                    eng.wait_ge(h, val)
                    waited[key] = val
                    nwait += 1
                if o.is_dma:
                    out, in_, kw = o.fn
                    eng.dma_start(out=out, in_=in_, **kw).then_inc(o.dsem.handle, 16)
                else:
                    ins = o.fn(eng)
                    if o.needs_inc:
                        ins.then_inc(sems[ename], 1)
            if ename == "sp":
                for d in self.dsems:
                    if d.count > 0:
                        eng.wait_ge(d.handle, d.count)
            self.stats[ename] = (len(self.ops[ename]), nwait)

        with nc.Block() as block:
            @block.tensor
            def _(eng):
                emit_engine("pe", eng)

            @block.scalar
            def _(eng):
                emit_engine("act", eng)

            @block.vector
            def _(eng):
                emit_engine("dve", eng)

            @block.gpsimd
            def _(eng):
                emit_engine("pool", eng)

            @block.sync
            def _(eng):
                emit_engine("sp", eng)

        for g in reversed(self.ctx):
            g.__exit__(None, None, None)
        self.ctx = []


D = 1024
SEQ = 2048
NS = 16
DK = 256
DV = 512
H = 4
DFF = 2816
D_IN = 10240
EPS = 1e-6
NCHS = 9
NT = NCHS * 128
GAM = [1.0 - 2.0 ** (-5 - h) for h in range(H)]
CD = [g ** 128 for g in GAM]
C_Q, C_K, C_V, C_G, C_GU, C_GV, C_AR, C_AG = 0, 1024, 2048, 4096, 6144, 7168, 8192, 9216


def _host_consts():
    c = {}
    c["ident"] = np.eye(128, dtype=np.float32).astype(ml_dtypes.bfloat16)
    s = np.arange(128)
    c["maskT"] = (s[None, :] >= s[:, None]).astype(np.float32)
    c["tril"] = (s[None, :] <= s[:, None]).astype(np.float32)
    dec = np.zeros((128, 8), np.float64)
    for h in range(H):
        dec[:, h] = GAM[h] ** (s + 1.0)
        dec[:, 4 + h] = GAM[h] ** (-(s + 1.0)) / 16.0
    c["dec"] = dec.astype(np.float32)
    inv = (np.float32(10000.0) ** (-np.arange(128, dtype=np.float32) / np.float32(128))).astype(np.float32)
    pos = np.concatenate([np.arange(SEQ, dtype=np.float32), np.full((128,), 16384.0, np.float32)])
    ang = (pos[:, None] * inv[None, :]).astype(np.float32)
    c["cs"] = np.concatenate([np.cos(ang), np.sin(ang)], axis=1).astype(np.float32)
    d16 = np.zeros((128, 16, 16), np.float32)
    d16[:, np.arange(16), np.arange(16)] = 1.0
    c["d16"] = d16.reshape(128, 256).astype(ml_dtypes.bfloat16)
    return c


def build_program():
    nc = bass.Bass("TRN2", target_bir_lowering=False)
    fw = FW(nc)

    def din(name, shape, dt=F32):
        return nc.dram_tensor(name, list(shape), dt, kind="ExternalInput").ap()

    def dout(name, shape, dt=F32):
        return nc.dram_tensor(name, list(shape), dt, kind="ExternalOutput").ap()

    xp = din("xp", [SEQ, D]); xs = din("xs", [NS, D]); st_in = din("st_in", [NS, H, DK, DV])
    w_in = din("w_in", [D, D_IN]); w_ret_o = din("w_ret_o", [2048, D]); w_gm_o = din("w_gm_o", [D, D])
    w_o = din("w_o", [D, D]); w_ffn_in = din("w_ffn_in", [D, 2 * DFF]); w_down = din("w_down", [DFF, D])
    g_mix = din("g_mix", [1, D]); g_ffn = din("g_ffn", [1, D]); g_fin = din("g_fin", [1, D])
    ln_g = din("ln_g", [1, D]); ln_b = din("ln_b", [1, D])
    gm_ws = din("gm_ws", [H, 128, 128]); gm_bs = din("gm_bs", [H, 128])
    c_ident = din("ident", [128, 128], BF16); c_maskT = din("maskT", [128, 128]); c_tril = din("tril", [128, 128])
    c_dec = din("dec", [128, 8]); c_cs = din("cs", [SEQ + 128, 256]); c_d16 = din("d16", [128, 256], BF16)
    yp = dout("yp", [SEQ, D]); ys = dout("ys", [NS, D]); rsp = dout("rsp", [H, DK, DV])
    rss = dout("rss", [NS, H, DK, DV]); gmv = dout("gmv", [NS, D])

    sb = fw.sbuf
    RA = sb("RA", [128, 8, NT], BF16)
    RB = sb("RB", [128, 16, NT], BF16)
    RC = sb("RC", [128, 8, NT], BF16)
    RD = sb("RD", [128, NCHS, D], BF16)
    Hf = RB[:].rearrange("p a b -> p (a b)").bitcast(F32).rearrange("p (c n) -> p c n", n=D)
    W = [sb("W%d" % i, [128, 8, 512], BF16) for i in range(6)]
    S2 = [sb("S%d" % h, [128, 2, DV], F32) for h in range(2)]
    Sb = sb("Sb", [128, 2, DV], BF16)
    GA = sb("GA", [128, D], F32); GB = sb("GB", [128, D], F32)
    T4 = [sb("T4%d" % i, [128, D], F32) for i in range(2)]
    T2 = [sb("T2%d" % i, [128, D], BF16) for i in range(3)]
    cst = [sb("cst%d" % i, [128, 256], F32) for i in range(2)]
    ident = sb("identt", [128, 128], BF16); maskT = sb("maskTt", [128, 128], F32)
    wsT = sb("wsT", [128, 4, 128], BF16); bsT = sb("bsT", [128, 4], F32)
    wsb = sb("wsb", [128, 4], F32); bsb = sb("bsb", [128, 4], F32)
    dec = sb("dect", [128, 8], F32); d16 = sb("d16t", [128, 256], BF16)
    mhalf = sb("mhalf", [128, 1], F32)
    stat = sb("stat", [128, 16], F32)
    qks = [sb("qks%d" % i, [128, 4, 128], F32) for i in range(2)]
    rtc = sb("rtc", [128, 4, 128], F32)
    qkr = [sb("qkr%d" % i, [128, 4, 128], BF16) for i in range(2)]
    kdt = [sb("kdt%d" % i, [128, 256], BF16) for i in range(2)]
    qkT = [sb("qkT%d" % i, [128, 4, 128], BF16) for i in range(2)]
    scm = sb("scm", [128, 128], BF16)
    xn0 = sb("xn0", [1, D], F32)
    xn0T = sb("xn0T", [128, 8], F32)
    one11 = sb("one11", [1, 1], F32)
    s00t = sb("s00t", [1, 8], F32)
    vb = [sb("vb%d" % i, [128, DV], BF16) for i in range(2)]
    sg = [sb("sg%d" % i, [128, DV], BF16) for i in range(2)]
    go = [sb("go%d" % i, [128, DV], BF16) for i in range(2)]
    vbs = sb("vbs", [128, DV], BF16); sgs = sb("sgs", [128, DV], BF16)
    ksel = sb("ksel", [16, 16, 128], BF16); qsel = sb("qsel", [128, 2, 16, 16], BF16)
    ksv = sb("ksv", [16, 2, 128], BF16)

    PS = [fw.psum("PS%d" % i, [128, 512], F32) for i in range(8)]
    TR = PS[6][:].bitcast(BF16).rearrange("p (a b) -> p a b", b=128)
    SC = PS[0][:, 0:128]

    class _DC:
        n = 0
    def new_dc():
        _DC.n += 1
        return fw.dsem("c%d" % _DC.n)
    ds_w = [fw.dsem("w%d" % i) for i in range(6)]
    ds_x = [fw.dsem("x%d" % i) for i in range(2)]; ds_cs = [fw.dsem("cs%d" % i) for i in range(2)]
    ds_g = [fw.dsem("ga"), fw.dsem("gb")]
    ds_y = [fw.dsem("y%d" % i) for i in range(2)]
    ds_so = [fw.dsem("so%d" % i) for i in range(H)]; ds_si = [fw.dsem("si%d" % i) for i in range(4)]; ds_sso = [fw.dsem("sso%d" % i) for i in range(4)]
    ds_sl = [fw.dsem("sl%d" % i) for i in range(2)]
    ds_gmv = fw.dsem("gmv")

    fw.children[("T4", 0)] = [("SD", 0), ("SD", 1)]
    fw.children[("T4", 1)] = [("SD", 2), ("SD", 3)]
    fw.children[("T2", 0)] = [("SBF", 0), ("SBF", 1)]
    fw.children[("T2", 1)] = [("SBF", 2), ("SBF", 3)]
    op = fw.op
    cnt = {"w": 0, "x": 0, "cs": 0, "y": 0, "si": 0}
    import os
    DEBUG = bool(os.environ.get("MK_DEBUG"))
    ds_dbg = fw.dsem("dbg")

    def dump(name, ap, shape, dt, keys):
        if not DEBUG:
            return
        t = dout(name, shape, dt)
        for a in range(shape[1]):
            fw.dma("sp", t[:, a, :], ap[:, a, :], ds_dbg, reads=keys)

    fw.dma("sp", ident[:], c_ident[:, :], new_dc(), writes=["ident"])
    fw.dma("sp", maskT[:], c_maskT[:, :], new_dc(), writes=["maskT"])
    fw.dma("sp", dec[:], c_dec[:, :], new_dc(), writes=["dec"])
    fw.dma("sp", d16[:], c_d16[:, :], new_dc(), writes=["d16"])
    fw.dma("sp", T4[0][:, 0:128], c_tril[:, :], new_dc(), writes=[("T4", 0)])
    fw.dma("sp", bsT[:], gm_bs.rearrange("g t -> t g"), new_dc(), writes=["bsT"], allow_slow_non_contiguous=True)
    fw.dma("sp", wsb[:], gm_ws[:, 0:1, 0:1].rearrange("g a b -> (a b) g").partition_broadcast(128), new_dc(), writes=["wsb"], allow_slow_non_contiguous=True)
    fw.dma("sp", bsb[:], gm_bs[:, 0:1].rearrange("g a -> a g").partition_broadcast(128), new_dc(), writes=["bsb"], allow_slow_non_contiguous=True)
    op("dve", lambda e: e.memset(mhalf[:], -0.5), writes=["mhalf"])
    for g in range(H):
        fw.dma("sp", T4[1][:, g * 128:(g + 1) * 128], gm_ws[g, :, :], new_dc(), writes=[("T4", 1)])
    for g in range(H):
        op("dve", lambda e, g=g: e.tensor_tensor(out=T2[0][:, g * 128:(g + 1) * 128], in0=T4[1][:, g * 128:(g + 1) * 128],
                                                   in1=T4[0][:, 0:128], op=ALU.mult), reads=[("T4", 1), ("T4", 0)], writes=[("T2", 0)])
    for g in range(H):
        op("pe", lambda e, g=g: e.transpose(TR[:, g, :], T2[0][:, g * 128:(g + 1) * 128], ident[:]),
           reads=[("T2", 0), "ident"], writes=[("PS", 6)])
    op("act", lambda e: e.copy(out=wsT[:], in_=TR[:, 0:4, :]), reads=[("PS", 6)], writes=["wsT"])

    def load_w(src_list):
        i = cnt["w"] % 6
        cnt["w"] += 1
        for (src, coff, nkc) in src_list:
            ncol = src.shape[1]
            fw.dma("pool", W[i][:, 0:nkc, coff:coff + ncol], src.rearrange("(kc p) n -> p kc n", p=128),
                   ds_w[i], writes=[("W", i)])
        return i

    def load_g(which, src):
        t = GA if which == 0 else GB
        fw.dma("sp", t[:], src[0:1, :].partition_broadcast(128), ds_g[which], writes=[("G", which)])

    def rstd_from_ss(ss_col, out_col, n):
        op("dve", lambda e: e.tensor_scalar(out=stat[:, out_col:out_col + 1], in0=stat[:, ss_col:ss_col + 1],
                                             scalar1=1.0 / n, scalar2=EPS, op0=ALU.mult, op1=ALU.add),
           reads=[("stat", ss_col)], writes=[("stat", out_col)])
        op("pool", lambda e: e.tensor_tensor(out=stat[:, out_col:out_col + 1], in0=stat[:, out_col:out_col + 1],
                                              in1=mhalf[:], op=ALU.pow),
           reads=[("stat", out_col), "mhalf"], writes=[("stat", out_col)])

    def sumsq(in_ap, in_keys, col):
        op("dve", lambda e: e.memset(stat[:, col:col + 1], 0.0), writes=[("stat", col)])
        op("act", lambda e: e.activation(out=T2[2][:, 0:in_ap.shape[1]], in_=in_ap, func=AF.Square,
                                          accum_out=stat[:, col:col + 1]),
           reads=list(in_keys) + [("stat", col)], writes=[("T2", 2), ("stat", col)])

    def norm_chain(src_ap, src_keys, gt_which, xb, out_ap=None, out_key=None):
        gt = GA if gt_which == 0 else GB
        if out_ap is None:
            out_ap, out_key = T2[xb][:], ("T2", xb)
        sumsq(src_ap, src_keys, 0)
        rstd_from_ss(0, 1, D)
        op("dve", lambda e: e.scalar_tensor_tensor(out=out_ap, in0=src_ap, scalar=stat[:, 1:2], in1=gt[:],
                                                    op0=ALU.mult, op1=ALU.mult),
           reads=list(src_keys) + [("stat", 1), ("G", gt_which)], writes=[out_key])

    def norm_tr(xb, dstR, dst_key, li, in_ap=None, in_key=None):
        if in_ap is None:
            in_ap, in_key = T2[xb][:], ("T2", xb)
        for kc in range(8):
            op("pe", lambda e, kc=kc: e.transpose(TR[:, kc, :], in_ap[:, kc * 128:(kc + 1) * 128], ident[:]),
               reads=[in_key, "ident"], writes=[("PS", 6)])
        op("act", lambda e: e.copy(out=dstR[:, :, li * 128:(li + 1) * 128], in_=TR[:, :, :]),
           reads=[("PS", 6)], writes=[(dst_key, li)])

    def load_x(gi, li):
        i = cnt["x"] % 2
        cnt["x"] += 1
        if li == 8:
            op("dve", lambda e: e.memset(T4[i][:], 0.0), writes=[("T4", i)])
            fw.dma("sp", T4[i][0:NS, :], xs[:, :], ds_x[i], writes=[("T4", i)])
        else:
            c = gi * 8 + li
            fw.dma("sp", T4[i][:], xp[c * 128:(c + 1) * 128, :], ds_x[i], writes=[("T4", i)])
        return i

    def proj(bank, lhs_R, lhs_key, li, slot, nkc=8, kc0=0, start=True, stop=True, ncol=512, coff=0):
        for kc in range(nkc):
            op("pe", lambda e, kc=kc: e.matmul(PS[bank][:, 0:ncol], lhs_R[:, kc0 + kc, li * 128:(li + 1) * 128],
                                                W[slot][:, kc, coff:coff + ncol],
                                                start=(start and kc == 0), stop=(stop and kc == nkc - 1)),
               reads=[(lhs_key, li), ("W", slot)], writes=[("PS", bank)])

    for gi in range(2):
        lis = list(range(9)) if gi == 0 else list(range(8))

        fw.mark('g%d P0' % gi)
        def head_blocks(h):
            a = load_w([(w_in[:, C_Q + h * DK:C_Q + (h + 1) * DK], 0, 8), (w_in[:, C_K + h * DK:C_K + (h + 1) * DK], 256, 8)])
            b = load_w([(w_in[:, C_V + h * DV:C_V + (h + 1) * DV], 0, 8)])
            c = load_w([(w_in[:, C_G + h * DV:C_G + (h + 1) * DV], 0, 8)])
            return a, b, c
        if gi == 0:
            load_g(0, g_mix)
            nxt = head_blocks(0)
            pend = None
            Wf = [W[i][:].rearrange("p a b -> p (a b)").bitcast(F32) for i in range(6)]
            pieces = [(kc, cb) for kc in range(8) for cb in range(4)]
            ds_t0 = [fw.dsem("t0%d" % i) for i in range(3)]

            def t0_load(g8):
                sl = 3 + g8 % 3
                for j in range(4):
                    kc, cb = pieces[g8 * 4 + j]
                    fw.dma("sp", Wf[sl][:, j * 512:(j + 1) * 512], w_in[kc * 128:(kc + 1) * 128, cb * 512:(cb + 1) * 512],
                           ds_t0[sl - 3], writes=[("W", sl)])

            def t0_mm(g8):
                sl = 3 + g8 % 3
                for j in range(4):
                    kc, cb = pieces[g8 * 4 + j]
                    op("pe", lambda e, j=j, kc=kc, cb=cb, sl=sl: e.matmul(PS[cb][0:1, :], xn0T[:, kc:kc + 1], Wf[sl][:, j * 512:(j + 1) * 512],
                                                                       start=(kc == 0), stop=(kc == 7)),
                       reads=["xn0T", ("W", sl)], writes=[("PS", cb)])

            for n_, li in enumerate(lis):
                xi = load_x(gi, li)
                norm_chain(T4[xi][:], [("T4", xi)], 0, n_ % 2)
                if n_ == 0:
                    op("dve", lambda e, xi=xi: e.scalar_tensor_tensor(out=xn0[:], in0=T4[xi][0:1, :], scalar=stat[0:1, 1:2], in1=GA[0:1, :],
                                                                      op0=ALU.mult, op1=ALU.mult),
                       reads=[("T4", xi), ("stat", 1), ("G", 0)], writes=["xn0"])
                    op("dve", lambda e: e.memset(one11[:], 1.0), writes=["one11"])
                    for kc in range(8):
                        op("pe", lambda e, kc=kc: e.matmul(PS[7][:, kc:kc + 1], xn0[0:1, kc * 128:(kc + 1) * 128], one11[:], start=True, stop=True),
                           reads=["xn0", "one11"], writes=[("PS", 7)])
                    op("act", lambda e: e.copy(out=xn0T[:], in_=PS[7][:, 0:8]), reads=[("PS", 7)], writes=["xn0T"])
                    for g8 in range(3):
                        t0_load(g8)
                else:
                    g8 = n_ - 1
                    if g8 < 8:
                        t0_mm(g8)
                        if g8 + 3 < 8:
                            t0_load(g8 + 3)
                if pend is not None:
                    norm_tr(pend[0], RA, "RA", pend[1])
                pend = (n_ % 2, li)
            norm_tr(pend[0], RA, "RA", pend[1])
            for j in range(2):
                op("act", lambda e, j=j: e.copy(out=xn0[:, j * 512:(j + 1) * 512], in_=PS[j][0:1, :]), reads=[("PS", j)], writes=["xn0"])
            for j in range(2):
                op("dve", lambda e, j=j: e.tensor_tensor(out=xn0[:, j * 512:(j + 1) * 512], in0=xn0[:, j * 512:(j + 1) * 512], in1=PS[2 + j][0:1, :], op=ALU.mult),
                   reads=["xn0", ("PS", 2 + j)], writes=["xn0"])
            op("dve", lambda e: e.reduce_sum(out=s00t[:, 0:4], in_=xn0[:].rearrange("p (h d) -> p h d", h=4), axis=AX.X),
               reads=["xn0"], writes=["s00"])
            op("dve", lambda e: e.tensor_scalar(out=s00t[:, 4:8], in0=s00t[:, 0:4], scalar1=1.0 / 16, scalar2=None, op0=ALU.mult),
               reads=["s00"], writes=["s00"])
        else:
            nxt = nxt_g1

        prompt = [li for li in lis if li != 8]
        has_sample = (8 in lis)
        for h in range(H):
            fw.mark('g%d P1 h%d' % (gi, h))
            sQK, sV, sG = nxt
            if h + 1 < H:
                nxt = head_blocks(h + 1)
            Sc = S2[h % 2]
            skey = ("S", h % 2)
            if gi == 0:
                op("dve", lambda e, Sc=Sc: e.memset(Sc[:], 0.0), writes=[skey])
            else:
                fw.dma("sp", Sc[:], rsp[h].rearrange("(hf p) v -> p hf v", p=128), ds_sl[h % 2], reads=[("rsp", h)], writes=[skey])
            op("pool", lambda e, Sc=Sc: e.tensor_copy(out=Sb[:], in_=Sc[:]), reads=[skey], writes=["Sb"])

            def stageA(li, p, h=h, sQK=sQK, sV=sV, sG=sG):
                sample = (li == 8)
                c = 16 if sample else gi * 8 + li
                ci = cnt["cs"] % 2
                cnt["cs"] += 1
                fw.dma("sp", cst[ci][:], c_cs[c * 128:(c + 1) * 128, :], ds_cs[ci], writes=[("cst", ci)])
                proj(0, RA, "RA", li, sQK)
                proj(2, RA, "RA", li, sV)
                proj(4, RA, "RA", li, sG)
                q_in = PS[0][:, 0:256].rearrange("p (a b) -> p a b", b=128)
                k_in = PS[0][:, 256:512].rearrange("p (a b) -> p a b", b=128)
                if sample:
                    op("act", lambda e: e.copy(out=qks[p][:, 0:2, :], in_=q_in), reads=[("PS", 0)], writes=[("qks", p)])
                    op("act", lambda e: e.mul(out=qks[p][:, 2:4, :], in_=k_in, mul=1.0 / 16), reads=[("PS", 0)], writes=[("qks", p)])
                    vt, vk, st_, sk = vbs, "vbs", sgs, "sgs"
                else:
                    op("act", lambda e: e.activation(out=qks[p][:, 0:2, :], in_=q_in, func=AF.Copy, scale=dec[:, h:h + 1]),
                       reads=[("PS", 0), "dec"], writes=[("qks", p)])
                    op("act", lambda e: e.activation(out=qks[p][:, 2:4, :], in_=k_in, func=AF.Copy, scale=dec[:, 4 + h:5 + h]),
                       reads=[("PS", 0), "dec"], writes=[("qks", p)])
                    vt, vk, st_, sk = vb[p], ("vb", p), sg[p], ("sg", p)
                op("act", lambda e: e.copy(out=vt[:], in_=PS[2][:]), reads=[("PS", 2)], writes=[vk])
                op("act", lambda e: e.activation(out=st_[:], in_=PS[4][:], func=AF.Silu), reads=[("PS", 4)], writes=[sk])
                cosb = cst[ci][:, 0:128].unsqueeze(1).to_broadcast([128, 4, 128])
                sinb = cst[ci][:, 128:256].unsqueeze(1).to_broadcast([128, 4, 128])
                op("dve", lambda e: e.tensor_tensor(out=rtc[:], in0=qks[p][:], in1=cosb, op=ALU.mult),
                   reads=[("qks", p), ("cst", ci)], writes=["rtc"])
                op("dve", lambda e: e.tensor_tensor(out=qks[p][:], in0=qks[p][:], in1=sinb, op=ALU.mult),
                   reads=[("qks", p), ("cst", ci)], writes=[("qks", p)])
                c4 = rtc[:].rearrange("p (a b) d -> p a b d", a=2)
                s4 = qks[p][:].rearrange("p (a b) d -> p a b d", a=2)
                o4 = qkr[p][:].rearrange("p (a b) d -> p a b d", a=2)
                op("dve", lambda e: e.tensor_tensor(out=o4[:, :, 0, :], in0=c4[:, :, 0, :], in1=s4[:, :, 1, :], op=ALU.subtract),
                   reads=["rtc", ("qks", p)], writes=[("qkr", p)])
                op("dve", lambda e: e.tensor_tensor(out=o4[:, :, 1, :], in0=s4[:, :, 0, :], in1=c4[:, :, 1, :], op=ALU.add),
                   reads=["rtc", ("qks", p)], writes=[("qkr", p)])
                if not sample:
                    op("act", lambda e: e.mul(out=kdt[p][:], in_=qkr[p][:, 2:4, :].rearrange("p a b -> p (a b)"), mul=CD[h]),
                       reads=[("qkr", p)], writes=[("kdt", p)])

            def stageD(li, p, h=h, np_=128):
                for j in range(4):
                    op("pe", lambda e, j=j: e.transpose(TR[:, j, :], go[p][:, j * 128:(j + 1) * 128], ident[:]), reads=[("go", p), "ident"], writes=[("PS", 6)])
                fence = [("H", q) for q in range(NCHS)] if (h == 0 and li == 0) else []
                op("act", lambda e: e.copy(out=RB[:, h * 4:(h + 1) * 4, li * 128:(li + 1) * 128], in_=TR[:, 0:4, :]),
                   reads=[("PS", 6)], writes=[("RB", li)] + fence)

            def o_evac(p, sgt, sgk, npo, bank):
                op("dve", lambda e: e.memset(stat[:, 2:3], 0.0), writes=[("stat", 2)])
                op("act", lambda e: e.activation(out=T2[2][0:npo, 0:512], in_=PS[bank][0:npo, :], func=AF.Square, accum_out=stat[0:npo, 2:3]),
                   reads=[("PS", bank), ("stat", 2)], writes=[("T2", 2), ("stat", 2)])
                rstd_from_ss(2, 3, DV)
                op("dve", lambda e: e.scalar_tensor_tensor(out=go[p][0:npo, :], in0=PS[bank][0:npo, :], scalar=stat[0:npo, 3:4], in1=sgt[0:npo, :],
                                                            op0=ALU.mult, op1=ALU.mult),
                   reads=[("PS", bank), ("stat", 3), sgk], writes=[("go", p)])

            def stageB(li, p):
                for j in range(4):
                    op("pe", lambda e, j=j: e.transpose(TR[:, j, :], qkr[p][:, j, :], ident[:]), reads=[("qkr", p), "ident"], writes=[("PS", 6)])
                op("act", lambda e: e.copy(out=qkT[p][:], in_=TR[:, 0:4, :]), reads=[("PS", 6)], writes=[("qkT", p)])

            def stageC1(li, p):
                for hf in range(2):
                    op("pe", lambda e, hf=hf: e.matmul(SC, qkT[p][:, 2 + hf, :], qkT[p][:, hf, :], start=(hf == 0), stop=(hf == 1)),
                       reads=[("qkT", p)], writes=[("PS", 0)])
                op("dve", lambda e: e.tensor_tensor(out=scm[:], in0=SC, in1=maskT[:], op=ALU.mult),
                   reads=[("PS", 0), "maskT"], writes=["scm"])
                if gi == 0 and li == 0:
                    op("dve", lambda e, h=h: e.tensor_copy(out=scm[0:1, 0:1], in_=s00t[0:1, 4 + h:5 + h]), reads=["s00", "scm"], writes=["scm"])

            def stageC3(li, p, h=h, Sc=Sc, skey=skey):
                for hf in range(2):
                    bk = 3 if hf == 0 else 7
                    op("pe", lambda e, hf=hf, bk=bk: e.matmul(PS[bk][:], kdt[p][:, hf * 128:(hf + 1) * 128], vb[p][:], start=True, stop=True),
                       reads=[("kdt", p), ("vb", p)], writes=[("PS", bk)])
                for hf in range(2):
                    bk = 3 if hf == 0 else 7
                    op("dve", lambda e, hf=hf, bk=bk: e.scalar_tensor_tensor(out=Sc[:, hf, :], in0=Sc[:, hf, :], scalar=CD[h], in1=PS[bk][:],
                                                                            op0=ALU.mult, op1=ALU.add),
                       reads=[skey, ("PS", bk)], writes=[skey])

            def stageC2(li, p, Sc=Sc, skey=skey):
                op("pe", lambda e: e.matmul(PS[5][:], scm[:], vb[p][:], start=True, stop=False), reads=["scm", ("vb", p)], writes=[("PS", 5)])
                for hf in range(2):
                    op("pe", lambda e, hf=hf: e.matmul(PS[5][:], qkT[p][:, hf, :], Sb[:, hf, :], start=False, stop=(hf == 1)),
                       reads=[("qkT", p), "Sb"], writes=[("PS", 5)])
                op("act", lambda e: e.copy(out=Sb[:], in_=Sc[:]), reads=[skey], writes=["Sb"])
                o_evac(p, sg[p], ("sg", p), 128, 5)

            def tile_bufs(k):
                i4 = k % 4
                sd = T4[i4 // 2][:, (i4 % 2) * 512:(i4 % 2) * 512 + 512]
                sbf = T2[i4 // 2][:, (i4 % 2) * 512:(i4 % 2) * 512 + 512]
                return i4, sd, sbf, ("SD", i4), ("SBF", i4)

            def sample_load(k, h=h):
                hf, b = tiles[k]
                i4, sd, sbf, kd_, kb_ = tile_bufs(k)
                fw.dma("sp", sd, st_in[b, h, hf * 128:(hf + 1) * 128, :], ds_si[i4], writes=[kd_])

            def sample_kv(k, h=h):
                hf, b = tiles[k]
                i4, sd, sbf, kd_, kb_ = tile_bufs(k)
                bk = 3 if k % 2 == 0 else 7
                op("pe", lambda e: e.matmul(PS[bk][:], ksel[:, b, :], vbs[0:16, :], start=True, stop=True),
                   reads=["ksel", "vbs"], writes=[("PS", bk)])
                op("dve", lambda e: e.scalar_tensor_tensor(out=sd, in0=sd, scalar=GAM[h], in1=PS[bk][:], op0=ALU.mult, op1=ALU.add),
                   reads=[kd_, ("PS", bk)], writes=[kd_])
                fw.dma("act", rss[b, h, hf * 128:(hf + 1) * 128, :], sd, ds_sso[i4], reads=[kd_])
                op("act", lambda e: e.copy(out=sbf, in_=sd), reads=[kd_], writes=[kb_])

            def sample_o(k, h=h):
                hf, b = tiles[k]
                i4, sd, sbf, kd_, kb_ = tile_bufs(k)
                first = (k == 0)
                last = (k == len(tiles) - 1)
                op("pe", lambda e: e.matmul(PS[1][0:16, :], qsel[:, hf, b, :], sbf, start=first, stop=last),
                   reads=["qsel", kb_], writes=[("PS", 1)])

            def make_ksel(hf, p):
                op("dve", lambda e: e.tensor_tensor(out=ksel[:, :, :], in0=qkr[p][0:16, 2 + hf, :].unsqueeze(1).to_broadcast([16, 16, 128]),
                                                    in1=ident[0:16, 0:16].unsqueeze(2).to_broadcast([16, 16, 128]), op=ALU.mult),
                   reads=[("qkr", p), "ident"], writes=["ksel"])

            tiles = []
            if has_sample:
                stageA(8, 1)
                for hf in range(2):
                    op("pe", lambda e, hf=hf: e.transpose(TR[:, hf, :], qkr[1][:, hf, :], ident[:]), reads=[("qkr", 1), "ident"], writes=[("PS", 6)])
                op("act", lambda e: e.copy(out=qkT[1][:, 0:2, :], in_=TR[:, 0:2, :]), reads=[("PS", 6)], writes=[("qkT", 1)])
                d16v = d16[:].rearrange("p (a b) -> p a b", a=16)
                for hf in range(2):
                    op("dve", lambda e, hf=hf: e.tensor_tensor(out=qsel[:, hf, :, :], in0=qkT[1][:, hf, 0:16].unsqueeze(1).to_broadcast([128, 16, 16]),
                                                               in1=d16v, op=ALU.mult),
                       reads=[("qkT", 1), "d16"], writes=["qsel"])
                op("dve", lambda e: e.tensor_copy(out=ksv[:], in_=qkr[1][0:16, 2:4, :]), reads=[("qkr", 1)], writes=["ksv"])
                tiles = [(hf, b) for hf in range(2) for b in range(NS)]

            def make_ksel2(hf):
                op("dve", lambda e: e.tensor_tensor(out=ksel[:, :, :], in0=ksv[:, hf, :].unsqueeze(1).to_broadcast([16, 16, 128]),
                                                    in1=ident[0:16, 0:16].unsqueeze(2).to_broadcast([16, 16, 128]), op=ALU.mult),
                   reads=["ksv", "ident"], writes=["ksel"])

            npr = len(prompt)
            per_it = (len(tiles) + npr - 1) // npr if tiles else 0
            for k in range(min(4, len(tiles))):
                sample_load(k)
            stageA(prompt[0], 0)
            prev = None
            tpos = 0
            pending_o = []
            for idx, li in enumerate(prompt):
                p = idx % 2
                stageB(li, p)
                if prev is not None:
                    stageC2(prev[0], prev[1])
                if idx + 1 < npr:
                    stageA(prompt[idx + 1], (idx + 1) % 2)
                stageC1(li, p)
                if prev is not None:
                    stageD(prev[0], prev[1])
                stageC3(li, p)
                prev = (li, p)
                for k in pending_o:
                    sample_o(k)
                pending_o = []
                for _ in range(per_it):
                    if tpos < len(tiles):
                        if tiles[tpos][1] == 0:
                            make_ksel2(tiles[tpos][0])
                        sample_kv(tpos)
                        if tpos + 4 < len(tiles):
                            sample_load(tpos + 4)
                        pending_o.append(tpos)
                        tpos += 1
            for k in pending_o:
                sample_o(k)
            stageC2(prev[0], prev[1])
            stageD(prev[0], prev[1])
            if has_sample:
                pz = npr % 2
                op("dve", lambda e: e.memset(go[pz][:], 0.0), writes=[("go", pz)])
                o_evac(pz, sgs, "sgs", NS, 1)
                stageD(8, pz)
            fw.dma("sp", rsp[h].rearrange("(hf p) v -> p hf v", p=128), Sc[:], ds_so[h], reads=[skey], writes=[("rsp", h)])

        if gi == 0:
            dump("dbg_go", RB[:], [128, 16, NT], BF16, [("RB", q) for q in range(NCHS)])
        fw.mark('g%d P2a' % gi)
        load_g(0, ln_g); load_g(1, ln_b)
        sGU = [load_w([(w_in[:, C_GU + j * 512:C_GU + (j + 1) * 512], 0, 8)]) for j in range(2)]
        sGV = [load_w([(w_in[:, C_GV + j * 512:C_GV + (j + 1) * 512], 0, 8)]) for j in range(2)]
        sWGM = [load_w([(w_gm_o[:, j * 512:(j + 1) * 512], 0, 8)]) for j in range(2)]
        ggu_b = [T2[0][:], qks[0][:].rearrange("p a b -> p (a b)").bitcast(BF16)]
        ggu_k = [("T2", 0), ("qks", 0)]
        vn_b = [T2[1][:], qks[1][:].rearrange("p a b -> p (a b)").bitcast(BF16)]
        vn_k = [("T2", 1), ("qks", 1)]
        junk41 = T4[1][:].bitcast(BF16)[:, 0:D]

        def p2a_Xpe(li):
            for j in range(2):
                proj(j, RA, "RA", li, sGU[j])
            for j in range(2):
                proj(2 + j, RA, "RA", li, sGV[j])

        def p2a_Xrest(li, q):
            sample = (li == 8)
            gu_t, gu_key, vn_t, vn_key = ggu_b[q], ggu_k[q], vn_b[q], vn_k[q]
            for j in range(2):
                op("act", lambda e, j=j: e.activation(out=gu_t[:, j * 512:(j + 1) * 512], in_=PS[j][:], func=AF.Gelu_apprx_tanh),
                   reads=[("PS", j)], writes=[gu_key])
            op("dve", lambda e: e.memset(stat[:, 4:8], 0.0), writes=[("stat", 4), ("stat", 6), ("stat", 7)])
            for j in range(2):
                op("act", lambda e, j=j: e.activation(out=T4[0][:, j * 512:(j + 1) * 512], in_=PS[2 + j][:], func=AF.Gelu_apprx_tanh,
                                                      accum_out=stat[:, 4 + j:5 + j]),
                   reads=[("PS", 2 + j), ("stat", 4)], writes=[("T4", 0), ("stat", 4)])
            op("act", lambda e: e.activation(out=vn_t, in_=T4[0][:], func=AF.Square, accum_out=stat[:, 6:7]),
               reads=[("T4", 0), ("stat", 6)], writes=[vn_key, ("stat", 6)])
            op("dve", lambda e: e.tensor_tensor(out=stat[:, 7:8], in0=stat[:, 4:5], in1=stat[:, 5:6], op=ALU.add), reads=[("stat", 4)], writes=[("stat", 7)])
            op("dve", lambda e: e.tensor_scalar(out=stat[:, 8:9], in0=stat[:, 7:8], scalar1=1.0 / D, scalar2=None, op0=ALU.mult),
               reads=[("stat", 7)], writes=[("stat", 8)])
            op("dve", lambda e: e.scalar_tensor_tensor(out=T4[1][:], in0=T4[0][:], scalar=stat[:, 8:9], in1=GA[:], op0=ALU.subtract, op1=ALU.mult),
               reads=[("T4", 0), ("stat", 8), ("G", 0)], writes=[("T4", 1)])
            op("dve", lambda e: e.tensor_tensor(out=stat[:, 10:11], in0=stat[:, 8:9], in1=stat[:, 8:9], op=ALU.mult), reads=[("stat", 8)], writes=[("stat", 10)])
            op("dve", lambda e: e.scalar_tensor_tensor(out=stat[:, 9:10], in0=stat[:, 6:7], scalar=1.0 / D, in1=stat[:, 10:11],
                                                        op0=ALU.mult, op1=ALU.subtract), reads=[("stat", 6), ("stat", 10)], writes=[("stat", 9)])
            op("dve", lambda e: e.tensor_scalar(out=stat[:, 9:10], in0=stat[:, 9:10], scalar1=EPS, scalar2=None, op0=ALU.add),
               reads=[("stat", 9)], writes=[("stat", 9)])
            op("pool", lambda e: e.tensor_tensor(out=stat[:, 9:10], in0=stat[:, 9:10], in1=mhalf[:], op=ALU.pow),
               reads=[("stat", 9), "mhalf"], writes=[("stat", 9)])
            if not sample:
                op("dve", lambda e: e.scalar_tensor_tensor(out=vn_t, in0=T4[1][:], scalar=stat[:, 9:10], in1=GB[:], op0=ALU.mult, op1=ALU.add),
                   reads=[("T4", 1), ("stat", 9), ("G", 1)], writes=[vn_key])
            else:
                op("dve", lambda e: e.scalar_tensor_tensor(out=T4[0][:], in0=T4[1][:], scalar=stat[:, 9:10], in1=GB[:], op0=ALU.mult, op1=ALU.add),
                   reads=[("T4", 1), ("stat", 9), ("G", 1)], writes=[("T4", 0)])
                fw.dma("sp", gmv[:, :], T4[0][0:NS, :], ds_gmv, reads=[("T4", 0)])

        def p2a_Ymm_pe(li, q):
            vn_t, vn_key = vn_b[q], vn_k[q]
            if li != 8:
                for g in range(4):
                    op("pe", lambda e, g=g: e.matmul(PS[4 + g // 2][:, (g % 2) * 256:(g % 2) * 256 + 256], wsT[:, g, :], vn_t[:, g * 256:(g + 1) * 256],
                                                     start=True, stop=True), reads=["wsT", vn_key], writes=[("PS", 4 + g // 2)])

        def p2a_Ymm_dve(li, q):
            sample = (li == 8)
            gu_t, gu_key, vn_t, vn_key = ggu_b[q], ggu_k[q], vn_b[q], vn_k[q]
            if not sample:
                for g in range(4):
                    op("dve", lambda e, g=g: e.scalar_tensor_tensor(out=RD[:, li, g * 256:(g + 1) * 256], in0=PS[4 + g // 2][:, (g % 2) * 256:(g % 2) * 256 + 256],
                                                                    scalar=bsT[:, g:g + 1], in1=gu_t[:, g * 256:(g + 1) * 256], op0=ALU.add, op1=ALU.mult),
                       reads=[("PS", 4 + g // 2), "bsT", gu_key], writes=[("RD", li)])
            else:
                for g in range(4):
                    op("dve", lambda e, g=g: e.tensor_scalar(out=T4[1][:, g * 256:(g + 1) * 256], in0=T4[0][:, g * 256:(g + 1) * 256],
                                                             scalar1=wsb[:, g:g + 1], scalar2=bsb[:, g:g + 1], op0=ALU.mult, op1=ALU.add),
                       reads=[("T4", 0), "wsb", "bsb"], writes=[("T4", 1)])
                op("dve", lambda e: e.tensor_tensor(out=RD[:, li, :], in0=T4[1][:], in1=gu_t, op=ALU.mult),
                   reads=[("T4", 1), gu_key], writes=[("RD", li)])

        def p2a_Ytr(li):
            for kc in range(8):
                op("pe", lambda e, kc=kc: e.transpose(TR[:, kc, :], RD[:, li, kc * 128:(kc + 1) * 128], ident[:]), reads=[("RD", li), "ident"], writes=[("PS", 6)])
            op("act", lambda e: e.copy(out=RC[:, :, li * 128:(li + 1) * 128], in_=TR[:, :, :]), reads=[("PS", 6)], writes=[("RC", li)])

        p2a_Xpe(lis[0])
        p2a_Xrest(lis[0], 0)
        for n_, li in enumerate(lis):
            nx = n_ + 1 < len(lis)
            if nx:
                p2a_Xpe(lis[n_ + 1])
            p2a_Ymm_pe(li, n_ % 2)
            if n_ > 0:
                p2a_Ytr(lis[n_ - 1])
            p2a_Ymm_dve(li, n_ % 2)
            if nx:
                p2a_Xrest(lis[n_ + 1], (n_ + 1) % 2)
        p2a_Ytr(lis[-1])

        if gi == 0:
            dump("dbg_gm", RC[:], [128, 8, NT], BF16, [("RC", q) for q in range(NCHS)])
        fw.mark('g%d P2b' % gi)
        sAGM = [load_w([(w_in[:, C_AG + j * 512:C_AG + (j + 1) * 512], 0, 8)]) for j in range(2)]
        sWRO0 = [load_w([(w_ret_o[0:1024, j * 512:(j + 1) * 512], 0, 8)]) for j in range(2)]
        for n_, li in enumerate(lis):
            bb = 4 * (n_ % 2)
            tp = n_ % 2
            for j in range(2):
                proj(bb + j, RC, "RC", li, sWGM[j])
            for j in range(2):
                proj(bb + 2 + j, RA, "RA", li, sAGM[j])
            for j in range(2):
                op("act", lambda e, j=j, bb=bb, tp=tp: e.activation(out=T4[tp][:, j * 512:(j + 1) * 512], in_=PS[bb + 2 + j][:], func=AF.Sigmoid),
                   reads=[("PS", bb + 2 + j)], writes=[("T4", tp)])
                op("dve", lambda e, j=j, li=li, bb=bb, tp=tp: e.tensor_tensor(out=RD[:, li, j * 512:(j + 1) * 512], in0=PS[bb + j][:], in1=T4[tp][:, j * 512:(j + 1) * 512], op=ALU.mult),
                   reads=[("PS", bb + j), ("T4", tp)], writes=[("RD", li)])

        if gi == 0:
            dump("dbg_mgm", RD[:], [128, NCHS, D], BF16, [("RD", q) for q in range(NCHS)])
        fw.mark('g%d P3a' % gi)
        sWRO = [sWRO0, [load_w([(w_ret_o[1024:2048, j * 512:(j + 1) * 512], 0, 8)]) for j in range(2)]]
        sAR = [load_w([(w_in[:, C_AR + j * 512:C_AR + (j + 1) * 512], 0, 8)]) for j in range(2)]
        for n_, li in enumerate(lis):
            bb = 4 * (n_ % 2)
            tp = n_ % 2
            for j in range(2):
                proj(bb + j, RB, "RB", li, sWRO[0][j], kc0=0, start=True, stop=False)
                proj(bb + j, RB, "RB", li, sWRO[1][j], kc0=8, start=False, stop=True)
                proj(bb + 2 + j, RA, "RA", li, sAR[j])
            for j in range(2):
                op("act", lambda e, j=j, bb=bb, tp=tp: e.activation(out=T4[tp][:, j * 512:(j + 1) * 512], in_=PS[bb + 2 + j][:], func=AF.Sigmoid),
                   reads=[("PS", bb + 2 + j)], writes=[("T4", tp)])
                op("dve", lambda e, j=j, bb=bb, tp=tp: e.tensor_tensor(out=T4[tp][:, j * 512:(j + 1) * 512], in0=PS[bb + j][:], in1=T4[tp][:, j * 512:(j + 1) * 512], op=ALU.mult),
                   reads=[("PS", bb + j), ("T4", tp)], writes=[("T4", tp)])
            op("dve", lambda e, li=li, tp=tp: e.tensor_tensor(out=RD[:, li, :], in0=T4[tp][:], in1=RD[:, li, :], op=ALU.add),
               reads=[("T4", tp), ("RD", li)], writes=[("RD", li)])

        if gi == 0:
            dump("dbg_m", RD[:], [128, NCHS, D], BF16, [("RD", q) for q in range(NCHS)])
        fw.mark('g%d P3b' % gi)
        sWO = [load_w([(w_o[:, j * 512:(j + 1) * 512], 0, 8)]) for j in range(2)]
        load_g(0, g_ffn)

        def ffn_in_blocks(part):
            f0 = part * 1024
            nf = min(1024, DFF - f0)
            blocks = []
            for j in range((nf + 511) // 512):
                ncol = min(512, nf - j * 512)
                sg_ = load_w([(w_ffn_in[:, f0 + j * 512:f0 + j * 512 + ncol], 0, 8)])
                su_ = load_w([(w_ffn_in[:, DFF + f0 + j * 512:DFF + f0 + j * 512 + ncol], 0, 8)])
                blocks.append((sg_, su_, ncol))
            return blocks
        blocks_next = ffn_in_blocks(0)
        mTb = [T2[0][:].rearrange("p (a b) -> p a b", b=128),
               qks[0][:].rearrange("p a b -> p (a b)").bitcast(BF16).rearrange("p (a b) -> p a b", b=128)]
        mTk = [("T2", 0), ("qks", 0)]
        TR7 = PS[7][:].bitcast(BF16).rearrange("p (a b) -> p a b", b=128)

        def p3b_tm(li, q):
            mt_, mk_ = mTb[q], mTk[q]
            for kc in range(8):
                op("pe", lambda e, kc=kc: e.transpose(TR7[:, kc, :], RD[:, li, kc * 128:(kc + 1) * 128], ident[:]), reads=[("RD", li), "ident"], writes=[("PS", 7)])
            op("act", lambda e: e.copy(out=mt_, in_=TR7[:, :, :]), reads=[("PS", 7)], writes=[mk_])

        def p3b_mm(li, q):
            mt_, mk_ = mTb[q], mTk[q]
            for j in range(2):
                for kc in range(8):
                    op("pe", lambda e, kc=kc, j=j, sl=sWO[j]: e.matmul(PS[4 + j][:], mt_[:, kc, :], W[sl][:, kc, :], start=(kc == 0), stop=(kc == 7)),
                       reads=[mk_, ("W", sWO[j])], writes=[("PS", 4 + j)])

        def p3b_post(li):
            xi = load_x(gi, li)
            for j in range(2):
                rbkeys = [("RB", q) for q in range(NCHS)] if (li == 0 and j == 0) else []
                op("dve", lambda e, j=j, xi=xi: e.tensor_tensor(out=Hf[:, li, j * 512:(j + 1) * 512], in0=PS[4 + j][:], in1=T4[xi][:, j * 512:(j + 1) * 512], op=ALU.add),
                   reads=[("PS", 4 + j), ("T4", xi)], writes=[("H", li)] + rbkeys)
            norm_chain(Hf[:, li, :], [("H", li)], 0, 1, out_ap=RD[:, li, :], out_key=("RD", li))

        p3b_tm(lis[0], 0)
        if len(lis) > 1:
            p3b_tm(lis[1], 1)
        p3b_mm(lis[0], 0)
        p3b_post(lis[0])
        for n_, li in enumerate(lis):
            if n_ + 1 < len(lis):
                p3b_mm(lis[n_ + 1], (n_ + 1) % 2)
            if n_ + 2 < len(lis):
                p3b_tm(lis[n_ + 2], n_ % 2)
            if n_ + 1 < len(lis):
                p3b_post(lis[n_ + 1])
            if n_ > 0:
                norm_tr(1, RA, "RA", lis[n_ - 1], in_ap=RD[:, lis[n_ - 1], :], in_key=("RD", lis[n_ - 1]))
        norm_tr(1, RA, "RA", lis[-1], in_ap=RD[:, lis[-1], :], in_key=("RD", lis[-1]))

        if gi == 0:
            dump("dbg_h", Hf, [128, NCHS, D], F32, [("H", q) for q in range(NCHS)])
            dump("dbg_hn", RA[:], [128, 8, NT], BF16, [("RA", q) for q in range(NCHS)])
        for part in range(3):
            fw.mark('g%d FFN%d' % (gi, part))
            f0 = part * 1024
            nf = min(1024, DFF - f0)
            nfg = nf // 128
            blocks = blocks_next
            sWD = [load_w([(w_down[f0:f0 + nf, j * 512:(j + 1) * 512], 0, nfg)]) for j in range(2)]
            ntok = len(lis) * 128
            tts = [(t0, min(512, ntok - t0)) for t0 in range(0, ntok, 512)]
            pair = 0
            for fg in range(nfg):
                sg_, su_, ncol = blocks[fg // 4]
                co = (fg % 4) * 128
                for (t0, tn) in tts:
                    bg, bu = [(0, 1), (2, 3), (4, 5)][pair % 3]
                    pair += 1
                    rkeys = [("RA", q) for q in range(t0 // 128, (t0 + tn) // 128)]
                    for kc in range(8):
                        op("pe", lambda e, kc=kc, bg=bg, sg_=sg_, co=co, t0=t0, tn=tn: e.matmul(PS[bg][:, 0:tn], W[sg_][:, kc, co:co + 128], RA[:, kc, t0:t0 + tn],
                                                                                                start=(kc == 0), stop=(kc == 7)),
                           reads=rkeys + [("W", sg_)], writes=[("PS", bg)])
                    for kc in range(8):
                        op("pe", lambda e, kc=kc, bu=bu, su_=su_, co=co, t0=t0, tn=tn: e.matmul(PS[bu][:, 0:tn], W[su_][:, kc, co:co + 128], RA[:, kc, t0:t0 + tn],
                                                                                                start=(kc == 0), stop=(kc == 7)),
                           reads=rkeys + [("W", su_)], writes=[("PS", bu)])
                    ti = pair % 2
                    op("act", lambda e, bg=bg, tn=tn, ti=ti: e.activation(out=T4[ti][:, 0:tn], in_=PS[bg][:, 0:tn], func=AF.Silu),
                       reads=[("PS", bg)], writes=[("T4", ti)])
                    op("dve", lambda e, bu=bu, tn=tn, t0=t0, fg=fg, ti=ti: e.tensor_tensor(out=RC[:, fg, t0:t0 + tn], in0=PS[bu][:, 0:tn], in1=T4[ti][:, 0:tn], op=ALU.mult),
                       reads=[("PS", bu), ("T4", ti)], writes=[("RC", q) for q in range(t0 // 128, (t0 + tn) // 128)])
            last = (part == 2)
            if not last:
                blocks_next = ffn_in_blocks(part + 1)
            elif gi == 0:
                load_g(0, g_mix)
                nxt_g1 = head_blocks(0)
            if part == 1:
                load_g(1, g_fin)
            st = {'n': 0, 'pend': None}
            fin_prev = [None]

            def fin_p0(li, st=st):
                    sumsq(Hf[:, li, :], [("H", li)], 0)
                    rstd_from_ss(0, 1, D)
                    yi = cnt["y"] % 2
                    cnt["y"] += 1
                    op("dve", lambda e, li=li, yi=yi: e.scalar_tensor_tensor(out=T4[yi][:], in0=Hf[:, li, :], scalar=stat[:, 1:2], in1=GB[:], op0=ALU.mult, op1=ALU.mult),
                       reads=[("H", li), ("stat", 1), ("G", 1)], writes=[("T4", yi)])
                    if li == 8:
                        fw.dma("sp", ys[:, :], T4[yi][0:NS, :], ds_y[yi], reads=[("T4", yi)])
                    else:
                        c = gi * 8 + li
                        fw.dma("sp", yp[c * 128:(c + 1) * 128, :], T4[yi][:], ds_y[yi], reads=[("T4", yi)])
                    if gi == 0 and li < 8:
                        xi = load_x(1, li)
                        norm_chain(T4[xi][:], [("T4", xi)], 0, st['n'] % 2)
                        if st['pend'] is not None:
                            norm_tr(st['pend'][0], RA, "RA", st['pend'][1])
                        st['pend'] = (st['n'] % 2, li)
                        st['n'] += 1

            for li in lis:
                for j in range(2):
                    bk = (0, 1, 2, 3)[(li % 2) * 2 + j]
                    for fg in range(nfg):
                        op("pe", lambda e, fg=fg, j=j, bk=bk, li=li, sl=sWD[j], nfg=nfg: e.matmul(PS[bk][:], RC[:, fg, li * 128:(li + 1) * 128], W[sl][:, fg, :],
                                                                              start=(fg == 0), stop=(fg == nfg - 1)),
                           reads=[("RC", li), ("W", sWD[j])], writes=[("PS", bk)])
                    op("dve", lambda e, j=j, bk=bk, li=li: e.tensor_tensor(out=Hf[:, li, j * 512:(j + 1) * 512], in0=PS[bk][:], in1=Hf[:, li, j * 512:(j + 1) * 512], op=ALU.add),
                       reads=[("PS", bk), ("H", li)], writes=[("H", li)])
                if last:
                    if fin_prev[0] is not None:
                        fin_p0(fin_prev[0])
                    fin_prev[0] = li
            if last:
                fin_p0(fin_prev[0])
            if last and gi == 0:
                norm_tr(st['pend'][0], RA, "RA", st['pend'][1])

        if gi == 0:
            for li in range(NCHS):
                pass
        fw.alias_fence = True

    fw.finish()
    return nc, fw


_CACHE = {}


def kernel(**inputs):
    f32 = np.float32
    consts = _host_consts()
    if "nc" not in _CACHE:
        _CACHE["nc"] = build_program()
    nc, fw = _CACHE["nc"]
    x_prompt = np.asarray(inputs["x_prompt"], f32)
    x_sample = np.asarray(inputs["x_sample"], f32)
    state_ret = np.asarray(inputs["state_ret"], f32)
    shared = {
        "w_in": np.ascontiguousarray(np.asarray(inputs["w_in"], f32)[0]),
        "w_ret_o": np.ascontiguousarray(np.asarray(inputs["w_ret_o"], f32)[0]),
        "w_gm_o": np.ascontiguousarray(np.asarray(inputs["w_gm_o"], f32)[0]),
        "w_o": np.ascontiguousarray(np.asarray(inputs["w_o"], f32)[0]),
        "w_ffn_in": np.ascontiguousarray(np.asarray(inputs["w_ffn_in"], f32)[0]),
        "w_down": np.ascontiguousarray(np.asarray(inputs["w_ffn_down"], f32)[0]),
        "g_mix": np.asarray(inputs["norm_mix_g"], f32).reshape(1, D),
        "g_ffn": np.asarray(inputs["norm_ffn_g"], f32).reshape(1, D),
        "g_fin": np.asarray(inputs["norm_final_g"], f32).reshape(1, D),
        "ln_g": np.asarray(inputs["gm_ln_g"], f32).reshape(1, D),
        "ln_b": np.asarray(inputs["gm_ln_b"], f32).reshape(1, D),
        "gm_ws": np.ascontiguousarray(np.asarray(inputs["gm_ws"], f32)[0]),
        "gm_bs": np.ascontiguousarray(np.asarray(inputs["gm_bs"], f32)[0]),
    }
    shared.update(consts)
    in_maps = []
    for c in range(8):
        m = dict(shared)
        m["xp"] = np.ascontiguousarray(x_prompt[c])
        m["xs"] = np.ascontiguousarray(x_sample[c * NS:(c + 1) * NS, 0, :])
        m["st_in"] = np.ascontiguousarray(state_ret[0, c * NS:(c + 1) * NS])
        in_maps.append(m)
    res = run_bass_kernel_spmd(nc, in_maps, core_ids=list(range(8)))
    r = res.results
    _CACHE["last"] = r
    y_prompt = np.stack([np.asarray(r[c]["yp"], f32) for c in range(8)], axis=0)
    y_sample = np.concatenate([np.asarray(r[c]["ys"], f32) for c in range(8)], axis=0).reshape(128, 1, D)
    ret_p = np.stack([np.asarray(r[c]["rsp"], f32) for c in range(8)], axis=0)[None]
    ret_s = np.concatenate([np.asarray(r[c]["rss"], f32) for c in range(8)], axis=0)[None]
    gm_v = np.concatenate([np.asarray(r[c]["gmv"], f32) for c in range(8)], axis=0).reshape(1, 128, 1, D)
    return (y_prompt, y_sample, ret_p, ret_s, gm_v)
```
